# Optimizing a Trainium2 kernel written in Bass

```python
import jax, jax.numpy as jnp
from jax import lax
import numpy as np

D_MODEL = 1024
BATCH = 2
SEQ = 16384
DEPTH = 2

HEAD_DIM = 64
HEADS_PER_GROUP = 4
ATTN_PAIRS = ((128, 1), (512, 4), (2048, 16))
N_GROUPS = 3
N_ATTN_HEADS = N_GROUPS * HEADS_PER_GROUP
ATTN_WIDTH = N_ATTN_HEADS * HEAD_DIM
ATTN_OUT_WIDTH = HEADS_PER_GROUP * HEAD_DIM
ATTN_BLOCK = 128
ALIBI_MAX_EXP = 8.0
CONF_WIDTH = 768
CONF_KERNEL = 31
SC_WIDTH = 768
SC_KERNEL = 3
N_BRANCHES = 3
D_FF = 2816
FFN_KERNEL = 3
NORM_EPS = 1e-6
IN_WIDTHS = (ATTN_WIDTH, ATTN_WIDTH, ATTN_WIDTH, CONF_WIDTH, CONF_WIDTH, SC_WIDTH, SC_WIDTH, SC_WIDTH, N_BRANCHES * D_MODEL)
IN_WIDTH = 3 * ATTN_WIDTH + 2 * CONF_WIDTH + 3 * SC_WIDTH + N_BRANCHES * D_MODEL

kernel_name = 'hybrid_dilated_attn_conformer_shortconv_block'


def rms_norm(x, g):
    xf = x.astype(jnp.float32)
    y = xf * lax.rsqrt(jnp.mean(xf * xf, axis=-1, keepdims=True) + NORM_EPS)
    return (y * g.astype(jnp.float32)).astype(x.dtype)


def layer_norm(x, g, b):
    xf = x.astype(jnp.float32)
    mu = jnp.mean(xf, axis=-1, keepdims=True)
    xc = xf - mu
    y = xc * lax.rsqrt(jnp.mean(xc * xc, axis=-1, keepdims=True) + NORM_EPS)
    return (y * g.astype(jnp.float32) + b.astype(jnp.float32)).astype(x.dtype)


def causal_dwconv(x, w):
    K, C = w.shape
    return lax.conv_general_dilated(
        x, w[:, None, :].astype(x.dtype), window_strides=(1,), padding=[(K - 1, 0)],
        dimension_numbers=('NWC', 'WIO', 'NWC'), feature_group_count=C)


def dilated_window_attention(q, k, v, window, dilation, slopes):
    B, S, H, hd = q.shape
    L = S // dilation
    span = window // dilation
    nb = -(-L // ATTN_BLOCK)
    Lp = nb * ATTN_BLOCK

    def to_sub(t):
        return t.reshape(B, L, dilation, H, hd).transpose(0, 2, 3, 1, 4)

    qb = jnp.pad(to_sub(q), ((0, 0), (0, 0), (0, 0), (0, Lp - L), (0, 0)))
    qb = qb.reshape(B, dilation, H, nb, ATTN_BLOCK, hd)

    def key_windows(t):
        tp = jnp.pad(to_sub(t), ((0, 0), (0, 0), (0, 0), (ATTN_BLOCK, Lp - L), (0, 0)))
        tp = tp.reshape(B, dilation, H, nb + 1, ATTN_BLOCK, hd)
        return jnp.concatenate([tp[:, :, :, :-1], tp[:, :, :, 1:]], axis=4)

    kw = key_windows(k)
    vw = key_windows(v)
    s = jnp.einsum('brhnqd,brhnkd->brhnqk', qb, kw, preferred_element_type=jnp.float32)
    s = s * (hd ** -0.5)
    qi = jnp.arange(ATTN_BLOCK)[:, None]
    ki = jnp.arange(2 * ATTN_BLOCK)[None, :]
    steps = qi + ATTN_BLOCK - ki
    blk = jnp.arange(nb)[:, None, None]
    valid = (steps >= 0) & (steps <= span) & (blk * ATTN_BLOCK + ki - ATTN_BLOCK >= 0)
    dist = (steps * dilation).astype(jnp.float32)
    s = s - slopes.astype(jnp.float32)[None, None, :, None, None, None] * dist
    s = jnp.where(valid, s, -jnp.inf)
    m = jnp.max(s, axis=-1, keepdims=True)
    p = jnp.exp(s - m)
    l = jnp.sum(p, axis=-1, keepdims=True)
    o = jnp.einsum('brhnqk,brhnkd->brhnqd', p, vw.astype(jnp.float32)) / l
    lse = (m + jnp.log(l))[..., 0]

    def from_sub(t):
        t = t.reshape((B, dilation, H, Lp) + t.shape[5:])[:, :, :, :L]
        t = jnp.moveaxis(t, 3, 1)
        return t.reshape((B, S, H) + t.shape[4:])

    return from_sub(o), from_sub(lse)


def hybrid_layer(x, norm1_g, w_in, conf_dw_w, conf_dw_b, conf_ln_g, conf_ln_b, w_conf_out,
                 sc_dw_w, w_sc_out, w_attn_out, w_o, norm2_g, w_up, ffn_dw_w, w_down):
    B, S, D = x.shape
    h = rms_norm(x, norm1_g)
    u = h @ w_in
    split_points = [int(c) for c in np.cumsum(IN_WIDTHS)[:-1]]
    q, k, v, conf_a, conf_gate, sc_b, sc_c, sc_x, gates = jnp.split(u, split_points, axis=-1)

    q = q.reshape(B, S, N_GROUPS, HEADS_PER_GROUP, HEAD_DIM)
    k = k.reshape(B, S, N_GROUPS, HEADS_PER_GROUP, HEAD_DIM)
    v = v.reshape(B, S, N_GROUPS, HEADS_PER_GROUP, HEAD_DIM)
    slopes = jnp.exp2(-ALIBI_MAX_EXP * jnp.arange(1, N_ATTN_HEADS + 1, dtype=jnp.float32) / N_ATTN_HEADS)
    slopes = slopes.reshape(N_GROUPS, HEADS_PER_GROUP)
    outs, lses = [], []
    for gi, (window, dilation) in enumerate(ATTN_PAIRS):
        o_g, lse_g = dilated_window_attention(q[:, :, gi], k[:, :, gi], v[:, :, gi], window, dilation, slopes[gi])
        outs.append(o_g)
        lses.append(lse_g)
    wts = jax.nn.softmax(jnp.stack(lses, axis=0), axis=0)
    o = jnp.sum(wts[..., None] * jnp.stack(outs, axis=0), axis=0)
    attn = o.reshape(B, S, ATTN_OUT_WIDTH).astype(x.dtype) @ w_attn_out

    a = conf_a * jax.nn.sigmoid(conf_gate)
    a = causal_dwconv(a, conf_dw_w) + conf_dw_b
    a = jax.nn.silu(layer_norm(a, conf_ln_g, conf_ln_b))
    conf = a @ w_conf_out

    short = (sc_b * causal_dwconv(sc_c * sc_x, sc_dw_w)) @ w_sc_out

    g = jax.nn.sigmoid(gates.reshape(B, S, N_BRANCHES, D))
    mixed = g[:, :, 0] * attn + g[:, :, 1] * conf + g[:, :, 2] * short
    x = x + mixed @ w_o

    h = rms_norm(x, norm2_g)
    up = causal_dwconv(h @ w_up, ffn_dw_w)
    f_gate, f_val = jnp.split(up, 2, axis=-1)
    return x + (jax.nn.silu(f_gate) * f_val) @ w_down


def setup_inputs(seed: int = 0) -> dict:
    key = jax.random.key(seed)
    ks = jax.random.split(key, 20)
    f32 = jnp.float32

    def dense(k_, shape, fan_in):
        return jax.random.normal(k_, shape, f32) * (fan_in ** -0.5)

    def gain(k_, shape):
        return 1.0 + 0.01 * jax.random.normal(k_, shape, f32)

    return {
        'x': jax.random.normal(ks[0], (BATCH, SEQ, D_MODEL), f32),
        'norm1_g': gain(ks[1], (DEPTH, D_MODEL)),
        'w_in': dense(ks[2], (DEPTH, D_MODEL, IN_WIDTH), D_MODEL),
        'conf_dw_w': dense(ks[3], (DEPTH, CONF_KERNEL, CONF_WIDTH), CONF_KERNEL),
        'conf_dw_b': 0.01 * jax.random.normal(ks[4], (DEPTH, CONF_WIDTH), f32),
        'conf_ln_g': gain(ks[5], (DEPTH, CONF_WIDTH)),
        'conf_ln_b': 0.01 * jax.random.normal(ks[6], (DEPTH, CONF_WIDTH), f32),
        'w_conf_out': dense(ks[7], (DEPTH, CONF_WIDTH, D_MODEL), CONF_WIDTH),
        'sc_dw_w': dense(ks[8], (DEPTH, SC_KERNEL, SC_WIDTH), SC_KERNEL),
        'w_sc_out': dense(ks[9], (DEPTH, SC_WIDTH, D_MODEL), SC_WIDTH),
        'w_attn_out': dense(ks[10], (DEPTH, ATTN_OUT_WIDTH, D_MODEL), ATTN_OUT_WIDTH),
        'w_o': dense(ks[11], (DEPTH, D_MODEL, D_MODEL), D_MODEL),
        'norm2_g': gain(ks[12], (DEPTH, D_MODEL)),
        'w_up': dense(ks[13], (DEPTH, D_MODEL, 2 * D_FF), D_MODEL),
        'ffn_dw_w': dense(ks[14], (DEPTH, FFN_KERNEL, 2 * D_FF), FFN_KERNEL),
        'w_down': dense(ks[15], (DEPTH, D_FF, D_MODEL), D_FF),
        'final_g': gain(ks[16], (D_MODEL,)),
    }


def reference(x, norm1_g, w_in, conf_dw_w, conf_dw_b, conf_ln_g, conf_ln_b, w_conf_out,
              sc_dw_w, w_sc_out, w_attn_out, w_o, norm2_g, w_up, ffn_dw_w, w_down, final_g):
    for layer in range(DEPTH):
        x = hybrid_layer(x, norm1_g[layer], w_in[layer], conf_dw_w[layer], conf_dw_b[layer],
                         conf_ln_g[layer], conf_ln_b[layer], w_conf_out[layer], sc_dw_w[layer],
                         w_sc_out[layer], w_attn_out[layer], w_o[layer], norm2_g[layer], w_up[layer],
                         ffn_dw_w[layer], w_down[layer])
    return rms_norm(x, final_g)
```

```python
import contextlib
import numpy as np
import concourse.bass as bass
import concourse.mybir as mybir
from concourse.bass_utils import run_bass_kernel_spmd

F32 = mybir.dt.float32
BF16 = mybir.dt.bfloat16
AF = mybir.ActivationFunctionType
ALU = mybir.AluOpType

D = 1024
T = 512
NCORES = 8
SEQ = 16384
SEG = SEQ // 4
DFF = 2816
INW = 9216
EPS = 1e-6
NEG = -30000.0
SLAB = 3072
NSLAB = 4

PP_N1G, PP_N2G, PP_CW, PP_CB, PP_LG, PP_LB, PP_SW, PP_FW, PP_FG, PP_N = 0, 8, 16, 202, 208, 214, 220, 238, 370, 378
C_ID, C_B12, C_B3A, C_B3B, C_N = 0, 128, 2176, 2688, 2816
DIL = (1, 4, 16)


DMA_SEMS = (["cst", "pp", "hv", "xld0", "xld1", "xst0", "xst1", "yst", "wst"] +
            [f"slab{i}" for i in range(NSLAB)] + [f"slabh{i}" for i in range(NSLAB)])


class Dep:
    __slots__ = ("w", "r")

    def __init__(self):
        self.w = None
        self.r = {}


class PB:
    def __init__(self, nc):
        self.nc = nc
        self.es = contextlib.ExitStack()
        self.engs = {"pe": nc.tensor, "dve": nc.vector, "act": nc.scalar, "pool": nc.gpsimd, "sp": nc.sync}
        self.sem = {}
        self.cnt = {}
        self.seen = {k: {} for k in self.engs}
        names = ["sem_" + k for k in self.engs] + ["dsem_" + n for n in DMA_SEMS]
        tmp = [nc.alloc_semaphore(name="pre_" + n) for n in names]
        nums = [t.num for t in tmp]
        nc.all_engine_barrier()
        nc.clear_and_free_semaphores(tmp)
        nc.all_engine_barrier()
        hs = [nc.alloc_semaphore(name=n, num=num) for n, num in zip(names, nums)]
        byname = dict(zip(names, hs))
        for k in self.engs:
            self.sem[k] = byname["sem_" + k]
            self.cnt[k] = 0
        for n in DMA_SEMS:
            self.sem["d:" + n] = byname["dsem_" + n]
            self.cnt["d:" + n] = 0
        self.nwait = 0
        self.nops = 0
        self.limit = None
        self.marks = []
        self.cur = 'init'
        self.labels = {k: [] for k in ('pe', 'dve', 'act', 'pool', 'sp')}

    def dsem(self, name):
        key = "d:" + name
        assert key in self.sem, name
        return key

    def sb(self, name, shape, dt):
        return self.es.enter_context(self.nc.sbuf_tensor(name, list(shape), dt))

    def mark(self, name):
        self.marks.append((name, self.nops))
        self.cur = name

    def _waits(self, eng, reads, writes):
        needs = {}
        for d in reads:
            if d.w is not None and needs.get(d.w[0], 0) < d.w[1]:
                needs[d.w[0]] = d.w[1]
        for d in writes:
            if d.w is not None and needs.get(d.w[0], 0) < d.w[1]:
                needs[d.w[0]] = d.w[1]
            for k, v in d.r.items():
                if needs.get(k, 0) < v:
                    needs[k] = v
        E = self.engs[eng]
        seen = self.seen[eng]
        for k, v in needs.items():
            if k == eng:
                if eng == "pe" or eng == "sp":
                    continue
            if seen.get(k, 0) >= v:
                continue
            assert v <= self.cnt[k], ("wait on unissued sync point", eng, k, v, self.cnt[k])
            seen[k] = v
            E.wait_ge(self.sem[k], v)
            self.nwait += 1

    def op(self, eng, fn, reads=(), writes=(), inc=True):
        self.nops += 1
        if self.limit is not None and self.nops > self.limit:
            return None
        self._waits(eng, reads, writes)
        ins = fn(self.engs[eng])
        self.labels[eng].append(self.cur)
        if inc:
            ins.then_inc(self.sem[eng], 1)
            self.cnt[eng] += 1
            tag = (eng, self.cnt[eng])
        else:
            tag = (eng, self.cnt[eng] + 1)
        for d in writes:
            d.w = tag
            d.r = {}
        for d in reads:
            if d.r.get(eng, 0) < tag[1]:
                d.r[eng] = tag[1]
        return ins

    def dma(self, q, out, in_, semname, reads=(), writes=()):
        key = self.dsem(semname)
        self.nops += 1
        if self.limit is not None and self.nops > self.limit:
            return None
        self._waits(q, reads, writes)
        ins = self.engs[q].dma_start(out=out, in_=in_)
        ins.then_inc(self.sem[key], 16)
        self.cnt[key] += 16
        tag = (key, self.cnt[key])
        for d in writes:
            d.w = tag
            d.r = {}
        for d in reads:
            if d.r.get(key, 0) < tag[1]:
                d.r[key] = tag[1]
        return ins

    def mm(self, out, lhsT, rhs, start, stop, reads, writes, inc=None, tp=None, sgc=False):
        if inc is None:
            inc = stop
        kw = {}
        if sgc:
            kw["skip_group_check"] = True
        if tp is not None:
            kw["tile_position"] = tp
        return self.op("pe", lambda E: E.matmul(out, lhsT=lhsT, rhs=rhs, start=start, stop=stop, **kw),
                       reads, writes, inc)

    def act(self, out, in_, func, reads, writes, scale=None, bias=None):
        kw = {}
        if scale is not None:
            kw["scale"] = scale
        if bias is not None:
            kw["bias"] = bias
        return self.op("act", lambda E: E.activation(out=out, in_=in_, func=func, **kw), reads, writes)

    def tt(self, eng, out, in0, in1, op, reads, writes):
        return self.op(eng, lambda E: E.tensor_tensor(out=out, in0=in0, in1=in1, op=op), reads, writes)

    def ts(self, eng, out, in0, s1, s2, op0, op1, reads, writes):
        if op1 is None:
            return self.op(eng, lambda E: E.tensor_scalar(out=out, in0=in0, scalar1=s1, scalar2=None, op0=op0),
                           reads, writes)
        return self.op(eng, lambda E: E.tensor_scalar(out=out, in0=in0, scalar1=s1, scalar2=s2, op0=op0, op1=op1),
                       reads, writes)

    def stt(self, out, in0, scalar, in1, op0, op1, reads, writes):
        return self.op("dve", lambda E: E.scalar_tensor_tensor(out=out, in0=in0, scalar=scalar, in1=in1,
                                                                 op0=op0, op1=op1), reads, writes)

    def copy(self, eng, out, in_, reads, writes):
        return self.op(eng, lambda E: E.tensor_copy(out=out, in_=in_), reads, writes)

    def memset(self, eng, ap, val, writes):
        return self.op(eng, lambda E: E.memset(ap, val), (), writes)


def make_plan(fused):
    if fused:
        s0 = dict(layer=0, tiles=[(-9, "kv3"), (-8, "kv3"), (-7, "kv3"), (-6, "kvall")] +
                  [(i, "full") for i in range(-5, 8)], src="xin", dst="scr", final=False)
        s1 = dict(layer=1, tiles=[(-5, "kv3"), (-4, "kv3"), (-3, "kv3"), (-2, "kvall")] +
                  [(i, "full") for i in range(-1, 8)], src="scr", dst="out", final=True)
        return [s0, s1], -9
    return None, None


def build_program(stages, tmin, n_in_tiles, n_layers_w=2, n_out_tiles=8, dbg=None):
    nc = bass.Bass("TRN2", target_bir_lowering=False)
    P = PB(nc)
    P.limit = dbg
    NTIN = n_in_tiles * T
    LW = n_layers_w

    def dram_in(name, shape):
        return nc.dram_tensor(name, list(shape), F32, kind="ExternalInput").ap()

    xin = dram_in("xin", [D, NTIN])
    w_in = dram_in("w_in", [LW, D, INW])
    w_conf_out = dram_in("w_conf_out", [LW, 768, D])
    w_sc_out = dram_in("w_sc_out", [LW, 768, D])
    w_attn_out = dram_in("w_attn_out", [LW, 256, D])
    w_o = dram_in("w_o", [LW, D, D])
    w_up = dram_in("w_up", [LW, D, 2 * DFF])
    w_down = dram_in("w_down", [LW, DFF, D])
    pp_d = dram_in("pp", [LW, 128, PP_N])
    cst_d = dram_in("cst", [128, C_N])
    hv_d = dram_in("hv", [128, 1])
    out_d = nc.dram_tensor("out", [D, n_out_tiles * T], F32, kind="ExternalOutput").ap()
    need_scr = any(s["dst"] == "scr" for s in stages)
    scr_d = nc.dram_tensor("scr", [D, NTIN], F32, kind="Internal").ap() if need_scr else None

    def fm(ap):
        return ap.rearrange("(kc p) t -> p kc t", p=128)

    xs = [P.sb("x0", [128, 8, T], F32), P.sb("x1", [128, 8, T], F32)]
    xds = [Dep(), Dep()]
    xi = {"i": 0}
    scr_all = Dep()
    h = P.sb("h", [128, 8, T], BF16)
    hd = Dep()
    qT = P.sb("qT", [128, 6, T], BF16)
    qd = Dep()
    kT1 = P.sb("kT1", [128, 2, 2, T], BF16)
    kT2 = P.sb("kT2", [128, 2, 2, T], BF16)
    kT3 = P.sb("kT3", [128, 2, 4 * T], BF16)
    k3c = P.sb("k3c", [128, 2, T], BF16)
    k1d, k2d, k3d, k3cd = Dep(), Dep(), Dep(), Dep()
    v1 = P.sb("v1", [128, 2, 4, 4, 65], BF16)
    v2 = P.sb("v2", [128, 2, 4, 4, 65], BF16)
    v3 = P.sb("v3", [128, 2, 16, 4, 65], BF16)
    v1d, v2d, v3d = Dep(), Dep(), Dep()
    NPT = 4
    pT = [P.sb(f"pT{i}", [128, T], BF16) for i in range(NPT)]
    pTd = [Dep() for _ in range(NPT)]
    oT = P.sb("oT", [64, 4, T], BF16)
    od = Dep()
    Y = P.sb("Y", [128, 8, T], F32)
    Yd = [Dep() for _ in range(8)]
    R = P.sb("R", [128, 22 * T], BF16)
    ACC = R[:, 0:16 * T].bitcast(F32).rearrange("p (a b) -> p a b", a=8)
    actT = R[:, :].rearrange("p (a b) -> p a b", a=22)
    Rd = Dep()
    accd = [Dep() for _ in range(8)]
    G = P.sb("G", [128, 8, T], BF16)
    Gd = [Dep() for _ in range(8)]
    a_buf = P.sb("a_buf", [128, 6, T + 30], BF16)
    ad = [Dep() for _ in range(6)]
    cx_buf = P.sb("cx_buf", [128, 6, T + 2], BF16)
    cxd = [Dep() for _ in range(6)]
    b_sc = P.sb("b_sc", [128, 6, T], BF16)
    bsd = [Dep() for _ in range(6)]
    NTF = 4
    tf = [P.sb(f"tf{i}", [128, T], F32) for i in range(NTF)]
    tfd = [Dep() for _ in range(NTF)]
    NDG = 16
    dg = [P.sb(f"dg{i}", [128, 128], BF16) for i in range(NDG)]
    dgd = [Dep() for _ in range(NDG)]
    uhist = P.sb("uhist", [128, 22, 2, 2], F32)
    uhd = [Dep() for _ in range(22)]
    ubhd = [Dep(), Dep()]
    cstb = P.sb("cstb", [128, C_N], BF16)
    cstd = Dep()
    onesb = P.sb("onesb", [128, 128], BF16)
    onesf = P.sb("onesf", [128, 128], F32)
    epsf = P.sb("epsf", [128, 1], F32)
    onescol = P.sb("onescol", [128, 64], BF16)
    constd = Dep()
    pp = P.sb("pp_sb", [128, LW, PP_N], F32)
    ppd = Dep()
    hv = P.sb("hv_sb", [128, 1], F32)
    hvd = Dep()
    slabs = [P.sb(f"slab{i}", [128, SLAB], BF16) for i in range(NSLAB)]
    slabd = [Dep() for _ in range(NSLAB)]
    ps = P.es.enter_context(nc.psum_tensor("ps", [128, 8, T], F32))
    psd = [Dep() for _ in range(8)]

    P.dma("pool", cstb[:, :], cst_d[:, :], "cst", writes=[cstd])
    P.dma("sp", pp[:, :, :], pp_d.rearrange("l p n -> p l n"), "pp", writes=[ppd])
    P.dma("sp", hv[:, :], hv_d[:, :], "hv", writes=[hvd])
    P.memset("dve", onesb[:, :], 1.0, [constd])
    P.memset("dve", onesf[:, :], 1.0, [constd])
    P.memset("dve", epsf[:, :], EPS, [constd])
    P.memset("dve", onescol[:, :], 1.0, [constd])
    def zero_state():
        for t_, d_ in ((kT1, k1d), (kT2, k2d), (kT3, k3d), (k3c, k3cd), (v1, v1d), (v2, v2d), (v3, v3d)):
            P.op("pool", lambda E, tt_=t_: E.memset(tt_[tuple(slice(None) for _ in tt_.shape)], 0.0), (), [d_])
        P.op("pool", lambda E: E.memset(a_buf[:, :, :], 0.0), (), ad)
        P.op("pool", lambda E: E.memset(cx_buf[:, :, :], 0.0), (), cxd)
        P.op("pool", lambda E: E.memset(uhist[:, :, :, :], 0.0), (), uhd)

    zero_state()
    for i_ in range(NPT):
        P.op("pool", lambda E, i_=i_: E.memset(pT[i_][:, :], 0.0), (), [pTd[i_]])

    def wsrc_of(kind):
        return {"in": w_in, "ao": w_attn_out, "co": w_conf_out, "so": w_sc_out, "wo": w_o,
                "up": w_up, "dn": w_down}[kind]

    def slab_parts(desc):
        kind, L = desc[0], desc[1]
        if kind == "in" or kind == "wo":
            c0, ncols = desc[2], desc[3]
            src = wsrc_of(kind)[L].rearrange("(kc p) c -> p kc c", p=128)[:, :, c0:c0 + ncols]
            n = 8 * ncols
            return [(lambda s_, n=n: s_[:, 0:n].rearrange("p (a b) -> p a b", a=8), src)]
        if kind in ("ao", "co", "so"):
            half = desc[2]
            rows, nk = (64, 4) if kind == "ao" else (128, 6)
            src = wsrc_of(kind)[L].rearrange("(kc p) c -> p kc c", p=rows)[:, :, half * 512:(half + 1) * 512]
            n = nk * 512
            return [(lambda s_, n=n, nk=nk, rows=rows: s_[0:rows, 0:n].rearrange("p (a b) -> p a b", a=nk), src)]
        if kind == "up":
            j = desc[2]
            srcu = w_up[L].rearrange("(kc p) c -> p kc c", p=128)
            return [
                (lambda s_: s_[:, 0:2048].rearrange("p (a b) -> p a b", a=8)[:, :, 0:128],
                 srcu[:, :, j * 128:(j + 1) * 128]),
                (lambda s_: s_[:, 0:2048].rearrange("p (a b) -> p a b", a=8)[:, :, 128:256],
                 srcu[:, :, DFF + j * 128:DFF + (j + 1) * 128]),
            ]
        if kind == "dn":
            oc = desc[2]
            srcd = w_down[L].rearrange("(kc p) c -> p kc c", p=128)[:, :, oc * 128:(oc + 1) * 128]
            return [(lambda s_: s_[:, 0:2816].rearrange("p (a b) -> p a b", a=22), srcd)]
        raise ValueError(kind)

    def tile_slabs(L, mode):
        sl = []
        if mode == "kv3":
            return [("in", L, 1280, 256), ("in", L, 2048, 256)]
        if mode == "full":
            sl += [("in", L, 0, 384), ("in", L, 384, 384)]
        sl += [("in", L, 768, 384), ("in", L, 1152, 384)]
        sl += [("in", L, 1536, 256), ("in", L, 1792, 256), ("in", L, 2048, 256)]
        for i in range(2):
            sl += [("in", L, 2304 + i * 384, 384), ("in", L, 3072 + i * 384, 384)]
        for i in range(2):
            if mode == "full":
                sl += [("in", L, 3840 + i * 384, 384)]
            sl += [("in", L, 4608 + i * 384, 384), ("in", L, 5376 + i * 384, 384)]
        if mode != "full":
            return sl
        for br, kind in enumerate(("ao", "co", "so")):
            sl += [("in", L, 6144 + br * 1024 + i * 256, 256) for i in range(4)]
            sl += [(kind, L, 0), (kind, L, 1)]
        sl += [("wo", L, 0, 384), ("wo", L, 384, 384), ("wo", L, 768, 256)]
        sl += [("up", L, j) for j in range(22)]
        sl += [("dn", L, oc) for oc in range(8)]
        return sl

    slab_seq = []
    for st_ in stages:
        for (_, mode_) in st_["tiles"]:
            slab_seq += tile_slabs(st_["layer"], mode_)
    slab_state = {"issued": 0, "consumed": 0}

    def slab_nel(desc):
        kind = desc[0]
        if kind in ("in", "wo"):
            return 8 * desc[3]
        if kind == "ao":
            return 4 * 512
        if kind in ("co", "so"):
            return 6 * 512
        if kind == "up":
            return 2048
        return 2816

    scr_off = {}
    tot = 0
    for d_ in slab_seq:
        if d_ not in scr_off:
            scr_off[d_] = tot
            tot += slab_nel(d_)
    wscr = nc.dram_tensor("wscr", [128, max(tot, 1)], BF16, kind="Internal").ap()
    scr_done = set()
    wst_all = Dep()

    def issue_slab(kk):
        d_ = slab_seq[kk]
        i = kk % NSLAB
        n = slab_nel(d_)
        rows = 64 if d_[0] == "ao" else 128
        off = scr_off[d_]
        if d_ in scr_done:
            P.dma("sp", slabs[i][0:rows, 0:n], wscr[0:rows, off:off + n], f"slabh{i}", reads=[wst_all],
                  writes=[slabd[i]])
        else:
            for (dst_fn, src) in slab_parts(d_):
                P.dma("pool", dst_fn(slabs[i]), src, f"slab{i}", writes=[slabd[i]])
            P.dma("sp", wscr[0:rows, off:off + n], slabs[i][0:rows, 0:n], "wst", reads=[slabd[i]],
                  writes=[wst_all])
            scr_done.add(d_)

    def next_slab(expect, held=0):
        k = slab_state["consumed"]
        assert slab_seq[k] == expect, (k, slab_seq[k], expect)
        lim = min(k - held + NSLAB, len(slab_seq))
        while slab_state["issued"] < lim:
            issue_slab(slab_state["issued"])
            slab_state["issued"] += 1
        assert slab_state["issued"] > k
        slab_state["consumed"] += 1
        i = k % NSLAB
        return slabs[i], slabd[i]

    def win_slab(L, c0, ncols, kind="in", held=0):
        t_, d_ = next_slab((kind, L, c0, ncols), held)
        n = 8 * ncols
        return t_[:, 0:n].rearrange("p (a b) -> p a b", a=8), d_

    bank_rr = {"i": 0}

    def nb(lo=0, hi=8):
        b = lo + bank_rr["i"] % (hi - lo)
        bank_rr["i"] += 1
        return b

    tf_rr = {"i": 0}

    def ntf():
        i = tf_rr["i"] % NTF
        tf_rr["i"] += 1
        return i

    def rmsnorm(L, gcol, out_t, out_deps, lpp, sq=None, sqd=None):
        if sq is None:
            sq, sqd = h, [hd]
        x, xd = xs[xi["i"]], xds[xi["i"]]
        P.act(sq[:, :, :], x[:, :, :], AF.Square, [xd], sqd)
        b = nb()
        for kc in range(8):
            P.mm(ps[:, b, :], onesb[:, :], sq[:, kc, :], kc == 0, kc == 7, list(sqd) + [constd], [psd[b]])
        t0 = ntf()
        P.ts("dve", tf[t0][:, :], ps[:, b, :], 1.0 / D, EPS, ALU.mult, ALU.add, [psd[b]], [tfd[t0]])
        P.act(tf[t0][:, :], tf[t0][:, :], AF.Sqrt, [tfd[t0]], [tfd[t0]])
        P.op("dve", lambda E: E.reciprocal(out=tf[t0][:, :], in_=tf[t0][:, :]), [tfd[t0]], [tfd[t0]])
        for kc in range(8):
            P.stt(out_t[:, kc, :], x[:, kc, :], pp[:, lpp, gcol + kc:gcol + kc + 1], tf[t0][:, :],
                  ALU.mult, ALU.mult, [xd, tfd[t0], ppd], out_deps)

    def proj_fm(wv, wd, c_local, rhs_t, rhs_d, bank, nk=8, rows=128):
        for kc in range(nk):
            P.mm(ps[:, bank, :], wv[0:rows, kc, c_local * 128:(c_local + 1) * 128], rhs_t[0:rows, kc, :],
                 kc == 0, kc == nk - 1, [wd, rhs_d], [psd[bank]])

    def resid(oc, b, halo):
        x, xd = xs[xi["i"]], xds[xi["i"]]
        if halo:
            P.stt(x[:, oc, :], ps[:, b, :], hv[:, 0:1], x[:, oc, :], ALU.mult, ALU.add, [psd[b], hvd, xd], [xd])
        else:
            P.tt("dve", x[:, oc, :], ps[:, b, :], x[:, oc, :], ALU.add, [psd[b], xd], [xd])

    def row_tiles(p0, p1):
        out = []
        p = p0
        while p < p1:
            if p % 128 == 0 and p1 - p >= 128:
                n = 128
            elif p % 64 == 0 and p1 - p >= 64:
                n = 64
            else:
                n = 32
            out.append((p, n))
            p += n
        return out

    def ones_col(vdep, sl_, halo, p0=0):
        npart = sl_.shape[0]
        if halo:
            n = sl_.shape[1] * sl_.shape[2]
            ov = onescol[p0:p0 + npart, 0:n].rearrange("p (a b c) -> p a b c", a=sl_.shape[1], c=1)
            P.ts("dve", sl_, ov, hv[p0:p0 + npart, 0:1], None, ALU.mult, None, [hvd, constd], [vdep])
        else:
            P.op("dve", lambda E: E.memset(sl_, 1.0), (), [vdep])

    def x_load(st, tidx, bi):
        tok0_ = (tidx - tmin) * T
        if st["src"] == "xin":
            P.dma("sp", xs[bi][:, :, :], fm(xin)[:, :, tok0_:tok0_ + T], f"xld{bi}", writes=[xds[bi]])
        else:
            P.dma("sp", xs[bi][:, :, :], fm(scr_d)[:, :, tok0_:tok0_ + T], f"xld{bi}", reads=[scr_all],
                  writes=[xds[bi]])

    def emit_tile(st, L, tidx, mode, gn, nxt, prenormed):
        halo = tidx < 0
        x, xd = xs[xi["i"]], xds[xi["i"]]
        tok0 = (tidx - tmin) * T
        par = gn % 2
        rho = 32 * (gn % 4)
        sc = (gn // 4) % 2
        P.mark("load_norm1")
        if not prenormed:
            rmsnorm(L, PP_N1G, h, [hd], L)
        if mode != "full" and nxt is not None:
            x_load(nxt[0], nxt[1], 1 - xi["i"])
        P.mark("qkv_proj")
        full = mode == "full"
        kvall = mode in ("full", "kvall")

        if full:
            for sl in range(2):
                wv, wd = win_slab(L, sl * 384, 384)
                for cl in range(3):
                    c = sl * 3 + cl
                    b = nb(0, 4)
                    proj_fm(wv, wd, cl, h, hd, b)
                    P.act(qT[:, c, :], ps[:, b, :], AF.Copy, [psd[b]], [qd], scale=0.125)
        if kvall:
            kplan = [(768, 384, [0, 1, 2]), (1152, 384, [3, 4, 5])]
        else:
            kplan = [(1280, 256, [4, 5])]
        for (c0, ncol, chunks) in kplan:
            wv, wd = win_slab(L, c0, ncol)
            for cl, c in enumerate(chunks):
                g, pair = c // 2, c % 2
                b = nb(0, 4)
                proj_fm(wv, wd, cl, h, hd, b)
                if g == 0:
                    P.copy("dve", kT1[:, pair, par, :], ps[:, b, :], [psd[b]], [k1d])
                elif g == 1:
                    P.copy("dve", kT2[:, pair, par, :], ps[:, b, :], [psd[b]], [k2d])
                else:
                    P.copy("dve", k3c[:, pair, :], ps[:, b, :], [psd[b]], [k3cd])
        if kvall:
            wv, wd = win_slab(L, 1536, 256)
            for j0 in (0, 2):
                b = nb(0, 4)
                for jj in range(2):
                    j = j0 + jj
                    for kc in range(8):
                        P.mm(ps[:, b, jj * 256:(jj + 1) * 256], h[:, kc, j * 128:(j + 1) * 128], wv[:, kc, :],
                             kc == 0, kc == 7, [hd, wd], [psd[b]])
                P.copy("dve", v1[:, par, j0:j0 + 2, :, 0:64],
                       ps[:, b, :].rearrange("p (a b c) -> p a b c", a=2, b=4), [psd[b]], [v1d])
            ones_col(v1d, v1[:, par, :, :, 64:65], halo)
            wv, wd = win_slab(L, 1792, 256)
            for r0 in (0, 2):
                b = nb(0, 4)
                for rr in range(2):
                    r = r0 + rr
                    for kc in range(8):
                        P.mm(ps[:, b, rr * 256:(rr + 1) * 256], h[:, kc, r:T:4], wv[:, kc, :],
                             kc == 0, kc == 7, [hd, wd], [psd[b]])
                P.copy("dve", v2[:, par, r0:r0 + 2, :, 0:64],
                       ps[:, b, :].rearrange("p (a b c) -> p a b c", a=2, b=4), [psd[b]], [v2d])
            ones_col(v2d, v2[:, par, :, :, 64:65], halo)
        wv, wd = win_slab(L, 2048, 256)
        for r0 in range(0, 16, 2):
            b = nb(0, 4)
            for rr in range(2):
                r = r0 + rr
                for kc in range(8):
                    P.mm(ps[rho:rho + 32, b, rr * 256:(rr + 1) * 256], h[:, kc, r:T:16], wv[:, kc, :],
                         kc == 0, kc == 7, [hd, wd], [psd[b]], tp=(0, rho))
            P.copy("dve", v3[rho:rho + 32, sc, r0:r0 + 2, :, 0:64],
                   ps[rho:rho + 32, b, :].rearrange("p (a b c) -> p a b c", a=2, b=4), [psd[b]], [v3d])
        ones_col(v3d, v3[rho:rho + 32, sc, :, :, 64:65], halo, rho)

        if kvall:
            conf_front(L, full)
            sc_front(L, full)
        if full:
            sc_conv(L)
        if not full:
            P.copy("pool", kT3[:, :, (gn % 4) * T:(gn % 4 + 1) * T], k3c[:, :, :], [k3cd], [k3d])
        else:
            attention(L, gn, par, rho, sc)
            conf_conv(L)
        P.op("pool", lambda E: E.memset(v3[rho:rho + 32, 1 - sc, :, :, :], 0.0), (), [v3d])
        if not full:
            return
        branch(L, 0)
        conf_back(L)
        branch(L, 1)
        branch(L, 2)
        P.mark("w_o")
        for (c0, ncol) in [(0, 384), (384, 384), (768, 256)]:
            wv, wd = win_slab(L, c0, ncol, kind="wo")
            for cl in range(ncol // 128):
                oc = c0 // 128 + cl
                b = nb()
                proj_fm(wv, wd, cl, h, hd, b)
                resid(oc, b, halo)
        if nxt is not None:
            x_load(nxt[0], nxt[1], 1 - xi["i"])
        P.mark("norm2")
        rmsnorm(L, PP_N2G, h, [hd], L)
        ffn(L, halo)
        did_pre = False
        if nxt is not None:
            P.mark("prenorm")
            xi["i"] = 1 - xi["i"]
            rmsnorm(nxt[2], PP_N1G, h, [hd], nxt[2])
            xi["i"] = 1 - xi["i"]
            did_pre = True
        ffn_down(L, halo)
        P.mark("out")
        bi = xi["i"]
        if st["dst"] == "scr":
            P.dma("sp", fm(scr_d)[:, :, tok0:tok0 + T], x[:, :, :], f"xst{bi}", reads=[xd], writes=[scr_all])
        elif tidx >= 0:
            if st["final"]:
                rmsnorm(L, PP_FG, Y, Yd, 0, sq=actT[:, 0:8, :], sqd=[Rd] + accd)
                P.dma("sp", fm(out_d)[:, :, tidx * T:(tidx + 1) * T], Y[:, :, :], "yst", reads=Yd)
            else:
                P.dma("sp", fm(out_d)[:, :, tidx * T:(tidx + 1) * T], x[:, :, :], f"xst{bi}", reads=[xd])
        return did_pre

    def attention(L, gn, par, rho, sc):
        P.mark("attention")
        items = [(g, hh, ch) for g in range(3) for hh in range(4) for ch in range(2)]
        nt = {"i": 0}

        def bias_ap(g, hh, ch):
            if g < 2:
                c0 = C_B12 + ((g * 2 + ch) * 4 + hh) * 128
                return cstb[:, c0:c0 + 128].unsqueeze(1).broadcast_to([128, 4, 128])
            if ch == 0:
                c0 = C_B3A + ((gn % 4) * 4 + hh) * 32
                return cstb[:, c0:c0 + 32].unsqueeze(1).broadcast_to([128, 16, 32])
            c0 = C_B3B + hh * 32
            return cstb[rho:rho + 32, c0:c0 + 32].unsqueeze(1).broadcast_to([32, 16, 32])

        def scores(g, hh, ch, b):
            pair, hr = hh // 2, 64 * (hh % 2)
            c = 2 * g + pair
            if g < 2:
                P.mm(ps[:, b, :].rearrange("p (u q) -> p u q", u=4), cstb[:, C_ID:C_ID + 128], bias_ap(g, hh, ch),
                     True, False, [cstd], [psd[b]], inc=False)
                for u in range(4):
                    if g == 0:
                        qv = qT[hr:hr + 64, c, u * 128:(u + 1) * 128]
                        if ch == 1:
                            kv = kT1[hr:hr + 64, pair, par, u * 128:(u + 1) * 128]
                        elif u > 0:
                            kv = kT1[hr:hr + 64, pair, par, (u - 1) * 128:u * 128]
                        else:
                            kv = kT1[hr:hr + 64, pair, 1 - par, 384:512]
                        kd = k1d
                    else:
                        qv = qT[hr:hr + 64, c, u:T:4]
                        kv = kT2[hr:hr + 64, pair, par if ch == 1 else 1 - par, u:T:4]
                        kd = k2d
                    P.mm(ps[:, b, u * 128:(u + 1) * 128], kv, qv, False, u == 3, [kd, qd], [psd[b]])
            else:
                if ch == 0:
                    P.mm(ps[:, b, :].rearrange("p (u q) -> p u q", u=16), cstb[:, C_ID:C_ID + 128],
                         bias_ap(g, hh, ch), True, False, [cstd], [psd[b]], inc=False)
                    for r in range(16):
                        P.mm(ps[:, b, r * 32:(r + 1) * 32], kT3[hr:hr + 64, pair, r:4 * T:16],
                             qT[hr:hr + 64, c, r:T:16], False, r == 15, [k3d, qd], [psd[b]])
                else:
                    P.mm(ps[rho:rho + 32, b, :].rearrange("p (u q) -> p u q", u=16),
                         cstb[rho:rho + 32, C_ID + rho:C_ID + rho + 32], bias_ap(g, hh, ch),
                         True, False, [cstd], [psd[b]], inc=False, tp=(rho, rho))
                    for r in range(16):
                        P.mm(ps[rho:rho + 32, b, r * 32:(r + 1) * 32], k3c[hr:hr + 64, pair, r:T:16],
                             qT[hr:hr + 64, c, r:T:16], False, r == 15, [k3cd, qd], [psd[b]], tp=(hr, rho))

        def expo(g, hh, ch, b, pi, bprev=None):
            if g == 2 and ch == 1:
                if rho > 0:
                    P.act(pT[pi][0:rho, :], ps[0:rho, bprev, :], AF.Exp, [psd[bprev]], [pTd[pi]])
                P.act(pT[pi][rho:rho + 32, :], ps[rho:rho + 32, b, :], AF.Exp, [psd[b]], [pTd[pi]])
            else:
                P.act(pT[pi][:, :], ps[:, b, :], AF.Exp, [psd[b]], [pTd[pi]])

        def pv(g, hh, ch, pi):
            nbk = 4 + hh
            if g == 0:
                for u in range(4):
                    if ch == 1:
                        vv = v1[:, par, u, hh, :]
                    elif u > 0:
                        vv = v1[:, par, u - 1, hh, :]
                    else:
                        vv = v1[:, 1 - par, 3, hh, :]
                    P.mm(ps[0:65, nbk, u * 128:(u + 1) * 128], vv, pT[pi][:, u * 128:(u + 1) * 128],
                         ch == 0 and u == 0, False, [v1d, pTd[pi]], [psd[nbk]], inc=(u == 3), sgc=True)
            elif g == 1:
                for u in range(4):
                    vv = v2[:, par if ch == 1 else 1 - par, u, hh, :]
                    P.mm(ps[0:65, nbk, u:T:4], vv, pT[pi][:, u * 128:(u + 1) * 128],
                         False, False, [v2d, pTd[pi]], [psd[nbk]], inc=(u == 3), sgc=True)
            else:
                slot = (1 - sc) if ch == 0 else sc
                for r in range(16):
                    P.mm(ps[0:65, nbk, r:T:16], v3[:, slot, r, hh, :], pT[pi][:, r * 32:(r + 1) * 32],
                         False, (ch == 1 and r == 15), [v3d, pTd[pi]], [psd[nbk]], inc=(r == 15), sgc=True)

        pend = []
        last_b = None
        for (g, hh, ch) in items:
            P.mark(f"att_g{g}")
            b = nb(0, 4)
            pi = nt["i"] % NPT
            nt["i"] += 1
            scores(g, hh, ch, b)
            expo(g, hh, ch, b, pi, last_b)
            last_b = b
            pend.append((g, hh, ch, pi))
            if len(pend) > 2:
                pv(*pend.pop(0))
        P.copy("pool", kT3[:, :, (gn % 4) * T:(gn % 4 + 1) * T], k3c[:, :, :], [k3cd], [k3d])
        while pend:
            pv(*pend.pop(0))

    def attn_norm():
        P.mark("attn_norm")
        rl = ACC[:, 0:4, :]
        rlb = ACC[:, 4:8, :]
        rld, rlbd = accd[0:4] + [Rd], accd[4:8] + [Rd]
        P.ts("dve", rl[64:65, :, :], ps[64:65, 4:8, :], 1e-30, None, ALU.add, None, psd[4:8], rld)
        P.op("dve", lambda E: E.reciprocal(out=rl[64:65, :, :], in_=rl[64:65, :, :]), rld, rld)
        for hh in range(4):
            P.mm(ps[0:64, hh, :], onesf[64:65, 0:64], rl[64:65, hh, :], True, True, [constd] + rld, [psd[hh]])
        P.act(rlb[0:64, :, :], ps[0:64, 0:4, :], AF.Copy, psd[0:4], rlbd)
        P.tt("dve", oT[:, :, :], ps[0:64, 4:8, :], rlb[0:64, :, :], ALU.mult, psd[4:8] + rlbd, [od])

    def conf_front(L, full):
        P.mark("conf_front")
        for c in range(6):
            P.copy("pool", a_buf[:, c, 0:30], a_buf[:, c, T:T + 30], [ad[c]], [ad[c]])
        for sl in range(2):
            wva, wda = win_slab(L, 2304 + sl * 384, 384)
            wvg, wdg = win_slab(L, 3072 + sl * 384, 384, held=1)
            for cl in range(3):
                c = sl * 3 + cl
                b1, b2 = nb(), nb()
                proj_fm(wvg, wdg, cl, h, hd, b1)
                proj_fm(wva, wda, cl, h, hd, b2)
                t0 = ntf()
                P.act(tf[t0][:, :], ps[:, b1, :], AF.Sigmoid, [psd[b1]], [tfd[t0]])
                P.tt("dve", a_buf[:, c, 30:30 + T], ps[:, b2, :], tf[t0][:, :], ALU.mult, [psd[b2], tfd[t0]], [ad[c]])

    def conf_conv(L):
        P.mark("conf_conv")
        n_ = 0
        for c in range(6):
            b = nb(0, 4)
            for k in range(31):
                i = n_ % NDG
                wcol = pp[:, L, PP_CW + c * 31 + k:PP_CW + c * 31 + k + 1]
                P.ts("dve", dg[i][:, :], cstb[:, C_ID:C_ID + 128], wcol, None, ALU.mult, None, [cstd, ppd], [dgd[i]])
                P.mm(ps[:, b, :], dg[i][:, :], a_buf[:, c, k:k + T], k == 0, k == 30, [dgd[i], ad[c]], [psd[b]],
                     inc=(n_ % 4 == 3 or k == 30))
                n_ += 1
            P.act(Y[:, c, :], ps[:, b, :], AF.Identity, [psd[b], ppd], [Yd[c]], bias=pp[:, L, PP_CB + c:PP_CB + c + 1])

    def conf_back(L):
        P.mark("conf_back")
        b1, b2 = nb(), nb()
        for c in range(6):
            P.mm(ps[:, b1, :], onesf[:, :], Y[:, c, :], c == 0, c == 5, [constd, Yd[c]], [psd[b1]])
        for c in range(6):
            t0 = ntf()
            P.act(tf[t0][:, :], Y[:, c, :], AF.Square, [Yd[c]], [tfd[t0]])
            P.mm(ps[:, b2, :], onesf[:, :], tf[t0][:, :], c == 0, c == 5, [constd, tfd[t0]], [psd[b2]], inc=True)
        m, rs = Y[:, 6, :], Y[:, 7, :]
        P.ts("dve", m, ps[:, b1, :], 1.0 / 768, None, ALU.mult, None, [psd[b1]], [Yd[6]])
        t0 = ntf()
        P.tt("dve", tf[t0][:, :], m, m, ALU.mult, [Yd[6]], [tfd[t0]])
        P.stt(rs, ps[:, b2, :], 1.0 / 768, tf[t0][:, :], ALU.mult, ALU.subtract, [psd[b2], tfd[t0]], [Yd[7]])
        P.act(rs, rs, AF.Sqrt, [Yd[7], constd], [Yd[7]], bias=epsf[:, 0:1])
        P.op("dve", lambda E: E.reciprocal(out=rs, in_=rs), [Yd[7]], [Yd[7]])
        tz = [ntf(), ntf()]
        for c in range(6):
            t0 = tz[c % 2]
            P.tt("dve", tf[t0][:, :], Y[:, c, :], m, ALU.subtract, [Yd[c], Yd[6]], [tfd[t0]])
            P.tt("dve", tf[t0][:, :], tf[t0][:, :], rs, ALU.mult, [tfd[t0], Yd[7]], [tfd[t0]])
            P.act(qT[:, c, :], tf[t0][:, :], AF.Silu, [tfd[t0], ppd], [qd],
                  scale=pp[:, L, PP_LG + c:PP_LG + c + 1], bias=pp[:, L, PP_LB + c:PP_LB + c + 1])

    def sc_front(L, full):
        P.mark("sc_front")
        for c in range(6):
            P.copy("pool", cx_buf[:, c, 0:2], cx_buf[:, c, T:T + 2], [cxd[c]], [cxd[c]])
        for sl in range(2):
            if full:
                wvb, wdb = win_slab(L, 3840 + sl * 384, 384)
            wvc, wdc = win_slab(L, 4608 + sl * 384, 384, held=1 if full else 0)
            wvx, wdx = win_slab(L, 5376 + sl * 384, 384, held=2 if full else 1)
            for cl in range(3):
                c = sl * 3 + cl
                if full:
                    b0 = nb()
                    proj_fm(wvb, wdb, cl, h, hd, b0)
                    P.act(b_sc[:, c, :], ps[:, b0, :], AF.Copy, [psd[b0]], [bsd[c]])
                b1, b2 = nb(), nb()
                proj_fm(wvc, wdc, cl, h, hd, b1)
                proj_fm(wvx, wdx, cl, h, hd, b2)
                t0 = ntf()
                P.act(tf[t0][:, :], ps[:, b1, :], AF.Copy, [psd[b1]], [tfd[t0]])
                P.tt("dve", cx_buf[:, c, 2:2 + T], ps[:, b2, :], tf[t0][:, :], ALU.mult, [psd[b2], tfd[t0]], [cxd[c]])

    def sc_conv(L):
        P.mark("sc_conv")
        for c0 in (0, 2, 4):
            tz = [ntf(), ntf()]
            for k in range(3):
                for cc in range(2):
                    c = c0 + cc
                    t0 = tz[cc]
                    wcol = pp[:, L, PP_SW + c * 3 + k:PP_SW + c * 3 + k + 1]
                    if k == 0:
                        P.ts("dve", tf[t0][:, :], cx_buf[:, c, 0:T], wcol, None, ALU.mult, None, [cxd[c], ppd], [tfd[t0]])
                    else:
                        P.stt(tf[t0][:, :], cx_buf[:, c, k:k + T], wcol, tf[t0][:, :], ALU.mult, ALU.add,
                              [cxd[c], ppd, tfd[t0]], [tfd[t0]])
            for cc in range(2):
                c = c0 + cc
                P.tt("dve", b_sc[:, c, :], b_sc[:, c, :], tf[tz[cc]][:, :], ALU.mult, [bsd[c], tfd[tz[cc]]], [bsd[c]])

    def branch(L, br):
        P.mark("branch")
        for sl in range(4):
            wv, wd = win_slab(L, 6144 + br * 1024 + sl * 256, 256)
            for cl in range(2):
                oc = sl * 2 + cl
                b = nb(0, 4) if br == 0 else nb()
                proj_fm(wv, wd, cl, h, hd, b)
                P.act(G[:, oc, :], ps[:, b, :], AF.Sigmoid, [psd[b]], [Gd[oc]])
        if br == 0:
            attn_norm()
            P.mark("branch")
            bkind, rows, nk, inp, inpd = "ao", 64, 4, oT, [od]
        elif br == 1:
            bkind, rows, nk, inp, inpd = "co", 128, 6, qT, [qd]
        else:
            bkind, rows, nk, inp, inpd = "so", 128, 6, b_sc, bsd
        for half in range(2):
            n = nk * 512
            t_, wd = next_slab((bkind, L, half))
            wv = t_[0:rows, 0:n].rearrange("p (a b) -> p a b", a=nk)
            for cl in range(4):
                oc = half * 4 + cl
                b = nb()
                for kc in range(nk):
                    P.mm(ps[:, b, :], wv[:, kc, cl * 128:(cl + 1) * 128], inp[0:rows, kc, :],
                         kc == 0, kc == nk - 1, [wd] + list(inpd), [psd[b]])
                if br == 0:
                    P.tt("dve", ACC[:, oc, :], ps[:, b, :], G[:, oc, :], ALU.mult, [psd[b], Gd[oc]], [accd[oc], Rd])
                else:
                    t0 = ntf()
                    P.tt("dve", tf[t0][:, :], ps[:, b, :], G[:, oc, :], ALU.mult, [psd[b], Gd[oc]], [tfd[t0]])
                    if br == 1:
                        P.tt("dve", ACC[:, oc, :], ACC[:, oc, :], tf[t0][:, :], ALU.add, [accd[oc], tfd[t0]],
                             [accd[oc], Rd])
                    else:
                        P.tt("dve", h[:, oc, :], ACC[:, oc, :], tf[t0][:, :], ALU.add, [accd[oc], tfd[t0]], [hd])

    def ffn(L, halo):
        P.mark("ffn")
        ub = [Y[:, 0:4, :], Y[:, 4:8, :]]
        fw = PP_FW
        fpages = [tf[i][:, :] for i in range(4)] + \
                 [G[:, 2 * i:2 * i + 2, :].rearrange("p a b -> p (a b)").bitcast(F32) for i in range(4)]
        fpaged = [[tfd[i]] for i in range(4)] + [[Gd[2 * i], Gd[2 * i + 1]] for i in range(4)]
        for j in range(22):
            s = j % 2
            ubv = ub[s].rearrange("p a b -> p (a b)").rearrange("p (e n) -> p e n", e=2)
            ubm2 = [[Yd[4 * s], Yd[4 * s + 1]], [Yd[4 * s + 2], Yd[4 * s + 3]]]
            ubh = ubhd[s]
            t_, wd = next_slab(("up", L, j))
            wv = t_[:, 0:2048].rearrange("p (a b) -> p a b", a=8)
            bg, bv = nb(), nb()
            proj_fm(wv, wd, 0, h, hd, bg)
            proj_fm(wv, wd, 1, h, hd, bv)
            P.copy("pool", ubv[:, :, 0:2], uhist[:, j, :, :], [uhd[j]],
                   [ubh] + (list(Yd[4 * s:4 * s + 4]) if j < 2 else []))
            P.act(ubv[:, 0, 2:2 + T], ps[:, bg, :], AF.Copy, [psd[bg]], ubm2[0])
            P.act(ubv[:, 1, 2:2 + T], ps[:, bv, :], AF.Copy, [psd[bv]], ubm2[1])
            P.copy("pool", uhist[:, j, :, :], ubv[:, :, T:T + 2], ubm2[0] + ubm2[1], [uhd[j]])
            pg_t, pg_d = fpages[(2 * j) % 8], fpaged[(2 * j) % 8]
            pv_t, pv_d = fpages[(2 * j + 1) % 8], fpaged[(2 * j + 1) % 8]
            P.act(pg_t, ps[:, bg, :], AF.Copy, [psd[bg], ppd], pg_d,
                  scale=pp[:, L, fw + j * 3 + 2:fw + j * 3 + 3])
            P.act(pv_t, ps[:, bv, :], AF.Copy, [psd[bv], ppd], pv_d,
                  scale=pp[:, L, fw + (22 + j) * 3 + 2:fw + (22 + j) * 3 + 3])
            for k in (1, 0):
                for e, tq, tqd, ch in ((0, pg_t, pg_d, j), (1, pv_t, pv_d, 22 + j)):
                    P.stt(tq, ubv[:, e, k:k + T], pp[:, L, fw + ch * 3 + k:fw + ch * 3 + k + 1], tq,
                          ALU.mult, ALU.add, ubm2[e] + [ubh, ppd] + tqd, tqd)
            P.act(pg_t, pg_t, AF.Silu, pg_d, pg_d)
            P.tt("dve", actT[:, j, :], pg_t, pv_t, ALU.mult, pg_d + pv_d, [Rd] + accd)

    def ffn_down(L, halo):
        P.mark("down")
        for oc in range(8):
            t_, wd = next_slab(("dn", L, oc))
            wv = t_[:, 0:2816].rearrange("p (a b) -> p a b", a=22)
            b = nb()
            for j in range(22):
                P.mm(ps[:, b, :], wv[:, j, :], actT[:, j, :], j == 0, j == 21, [wd, Rd], [psd[b]])
            resid(oc, b, halo)

    flat = []
    for si_, st in enumerate(stages):
        for gn, (tidx, mode) in enumerate(st["tiles"]):
            flat.append((si_, st, tidx, mode, gn))
    x_load(flat[0][1], flat[0][2], 0)
    pre = False
    for n_, (si_, st, tidx, mode, gn) in enumerate(flat):
        if gn == 0 and si_ > 0:
            zero_state()
        xi["i"] = n_ % 2
        nxt = (flat[n_ + 1][1], flat[n_ + 1][2], flat[n_ + 1][1]["layer"]) if n_ + 1 < len(flat) else None
        pre = bool(emit_tile(st, st["layer"], tidx, mode, gn, nxt, pre))
    for e_, E_ in P.engs.items():
        for k_, v_ in P.cnt.items():
            if v_ > 0 and k_ != e_:
                E_.wait_ge(P.sem[k_], v_)
    nc.all_engine_barrier()
    nc.clear_and_free_semaphores(list(P.sem.values()))
    nc.all_engine_barrier()
    P.es.close()
    return nc, P


def _pack_pp(inp, L):
    def v6(a):
        return a.reshape(-1, 128).T
    cols = [v6(inp["norm1_g"][L]), v6(inp["norm2_g"][L]),
            inp["conf_dw_w"][L].reshape(31, 6, 128).transpose(2, 1, 0).reshape(128, 186),
            v6(inp["conf_dw_b"][L]), v6(inp["conf_ln_g"][L]), v6(inp["conf_ln_b"][L]),
            inp["sc_dw_w"][L].reshape(3, 6, 128).transpose(2, 1, 0).reshape(128, 18),
            inp["ffn_dw_w"][L].reshape(3, 44, 128).transpose(2, 1, 0).reshape(128, 132),
            v6(inp["final_g"])]
    out = np.concatenate(cols, axis=1).astype(np.float32)
    assert out.shape == (128, PP_N)
    return out


def _make_cst():
    c = np.zeros((128, C_N), np.float32)
    c[:, C_ID:C_ID + 128] = np.eye(128, dtype=np.float32)
    slopes = 2.0 ** (-8.0 * np.arange(1, 13, dtype=np.float64) / 12.0)
    ki = np.arange(128)[:, None]
    qi = np.arange(128)[None, :]
    stepsA = qi + 128 - ki
    stepsB = qi - ki
    for g in range(3):
        d = DIL[g]
        for hh in range(4):
            sl = slopes[g * 4 + hh]
            bA = np.where(stepsA <= 128, -sl * d * stepsA, NEG)
            bB = np.where(stepsB >= 0, -sl * d * stepsB, NEG)
            if g < 2:
                c0 = C_B12 + ((g * 2 + 0) * 4 + hh) * 128
                c[:, c0:c0 + 128] = bA
                c0 = C_B12 + ((g * 2 + 1) * 4 + hh) * 128
                c[:, c0:c0 + 128] = bB
            else:
                for ri in range(4):
                    rows = (np.arange(128) - 32 * ri) % 128
                    c0 = C_B3A + (ri * 4 + hh) * 32
                    c[:, c0:c0 + 32] = bA[rows, 0:32]
                c0 = C_B3B + hh * 32
                c[:, c0:c0 + 32] = bB[np.arange(128) % 32, 0:32]
    return c


def _weights_map(inp):
    f = lambda a: np.ascontiguousarray(np.asarray(a, dtype=np.float32))
    m = {k: f(inp[k]) for k in ("w_in", "w_conf_out", "w_sc_out", "w_attn_out", "w_o", "w_up", "w_down")}
    m["pp"] = np.stack([_pack_pp(inp, L) for L in range(2)], axis=0)
    m["cst"] = _make_cst()
    return m


def _core_xin(xfull, core, tmin, n_in_tiles):
    b, s = core // 4, (core % 4) * SEG
    lo = s + tmin * T
    hi = lo + n_in_tiles * T
    out = np.zeros((n_in_tiles * T, D), np.float32)
    a = max(lo, 0)
    out[a - lo:hi - lo] = xfull[b, a:hi]
    return np.ascontiguousarray(out.T)


_CACHE = {}


def _run(stages, tmin, n_in_tiles, xfull, wm):
    key = repr((stages, tmin, n_in_tiles))
    if key not in _CACHE:
        _CACHE[key] = build_program(stages, tmin, n_in_tiles)[0]
    nc = _CACHE[key]
    in_maps = []
    for core in range(NCORES):
        m = dict(wm)
        m["xin"] = _core_xin(xfull, core, tmin, n_in_tiles)
        m["hv"] = np.full((128, 1), 0.0 if core % 4 == 0 else 1.0, np.float32)
        in_maps.append(m)
    res = run_bass_kernel_spmd(nc, in_maps, core_ids=list(range(NCORES)))
    out = np.zeros((2, SEQ, D), np.float32)
    for core in range(NCORES):
        b, s = core // 4, (core % 4) * SEG
        out[b, s:s + SEG] = res.results[core]["out"].T
    return out


FUSED = True


def kernel(**inputs):
    inp = {k: np.asarray(v) for k, v in inputs.items()}
    x = np.ascontiguousarray(inp["x"], dtype=np.float32)
    wm = _weights_map(inp)
    if FUSED:
        stages, tmin = make_plan(True)
        return _run(stages, tmin, 17, x, wm)
    tiles = [(-5, "kv3"), (-4, "kv3"), (-3, "kv3"), (-2, "kvall")] + [(i, "full") for i in range(-1, 8)]
    s0 = [dict(layer=0, tiles=tiles, src="xin", dst="out", final=False)]
    s1 = [dict(layer=1, tiles=tiles, src="xin", dst="out", final=True)]
    x1 = _run(s0, -5, 13, x, wm)
    return _run(s1, -5, 13, x1, wm)
```

```python
import contextlib
import numpy as np
import concourse.bass as bass
import concourse.mybir as mybir
from concourse.bass_utils import run_bass_kernel_spmd

F32 = mybir.dt.float32
BF16 = mybir.dt.bfloat16
AF = mybir.ActivationFunctionType
ALU = mybir.AluOpType

D = 1024
T = 512
NCORES = 8
SEQ = 16384
SEG = SEQ // 4
DFF = 2816
INW = 9216
EPS = 1e-6
NEG = -30000.0
SLAB = 3072
NSLAB = 4

PP_N1G, PP_N2G, PP_CW, PP_CB, PP_LG, PP_LB, PP_SW, PP_FW, PP_FG, PP_N = 0, 8, 16, 202, 208, 214, 220, 238, 370, 378
C_ID, C_B12, C_B3A, C_B3B, C_N = 0, 128, 2176, 2688, 2816
DIL = (1, 4, 16)


DMA_SEMS = (["cst", "pp", "hv", "xld0", "xld1", "xst0", "xst1", "yst", "wst"] +
            [f"slab{i}" for i in range(NSLAB)] + [f"slabh{i}" for i in range(NSLAB)])


class Dep:
    __slots__ = ("w", "r")

    def __init__(self):
        self.w = None
        self.r = {}


class PB:
    def __init__(self, nc):
        self.nc = nc
        self.es = contextlib.ExitStack()
        self.engs = {"pe": nc.tensor, "dve": nc.vector, "act": nc.scalar, "pool": nc.gpsimd, "sp": nc.sync}
        self.sem = {}
        self.cnt = {}
        self.seen = {k: {} for k in self.engs}
        names = ["sem_" + k for k in self.engs] + ["dsem_" + n for n in DMA_SEMS]
        tmp = [nc.alloc_semaphore(name="pre_" + n) for n in names]
        nums = [t.num for t in tmp]
        nc.all_engine_barrier()
        nc.clear_and_free_semaphores(tmp)
        nc.all_engine_barrier()
        hs = [nc.alloc_semaphore(name=n, num=num) for n, num in zip(names, nums)]
        byname = dict(zip(names, hs))
        for k in self.engs:
            self.sem[k] = byname["sem_" + k]
            self.cnt[k] = 0
        for n in DMA_SEMS:
            self.sem["d:" + n] = byname["dsem_" + n]
            self.cnt["d:" + n] = 0
        self.nwait = 0
        self.nops = 0
        self.limit = None
        self.marks = []
        self.cur = 'init'
        self.labels = {k: [] for k in ('pe', 'dve', 'act', 'pool', 'sp')}

    def dsem(self, name):
        key = "d:" + name
        assert key in self.sem, name
        return key

    def sb(self, name, shape, dt):
        return self.es.enter_context(self.nc.sbuf_tensor(name, list(shape), dt))

    def mark(self, name):
        self.marks.append((name, self.nops))
        self.cur = name

    def _waits(self, eng, reads, writes):
        needs = {}
        for d in reads:
            if d.w is not None and needs.get(d.w[0], 0) < d.w[1]:
                needs[d.w[0]] = d.w[1]
        for d in writes:
            if d.w is not None and needs.get(d.w[0], 0) < d.w[1]:
                needs[d.w[0]] = d.w[1]
            for k, v in d.r.items():
                if needs.get(k, 0) < v:
                    needs[k] = v
        E = self.engs[eng]
        seen = self.seen[eng]
        for k, v in needs.items():
            if k == eng:
                if eng == "pe" or eng == "sp":
                    continue
            if seen.get(k, 0) >= v:
                continue
            assert v <= self.cnt[k], ("wait on unissued sync point", eng, k, v, self.cnt[k])
            seen[k] = v
            E.wait_ge(self.sem[k], v)
            self.nwait += 1

    def op(self, eng, fn, reads=(), writes=(), inc=True):
        self.nops += 1
        if self.limit is not None and self.nops > self.limit:
            return None
        self._waits(eng, reads, writes)
        ins = fn(self.engs[eng])
        self.labels[eng].append(self.cur)
        if inc:
            ins.then_inc(self.sem[eng], 1)
            self.cnt[eng] += 1
            tag = (eng, self.cnt[eng])
        else:
            tag = (eng, self.cnt[eng] + 1)
        for d in writes:
            d.w = tag
            d.r = {}
        for d in reads:
            if d.r.get(eng, 0) < tag[1]:
                d.r[eng] = tag[1]
        return ins

    def dma(self, q, out, in_, semname, reads=(), writes=()):
        key = self.dsem(semname)
        self.nops += 1
        if self.limit is not None and self.nops > self.limit:
            return None
        self._waits(q, reads, writes)
        ins = self.engs[q].dma_start(out=out, in_=in_)
        ins.then_inc(self.sem[key], 16)
        self.cnt[key] += 16
        tag = (key, self.cnt[key])
        for d in writes:
            d.w = tag
            d.r = {}
        for d in reads:
            if d.r.get(key, 0) < tag[1]:
                d.r[key] = tag[1]
        return ins

    def mm(self, out, lhsT, rhs, start, stop, reads, writes, inc=None, tp=None, sgc=False):
        if inc is None:
            inc = stop
        kw = {}
        if sgc:
            kw["skip_group_check"] = True
        if tp is not None:
            kw["tile_position"] = tp
        return self.op("pe", lambda E: E.matmul(out, lhsT=lhsT, rhs=rhs, start=start, stop=stop, **kw),
                       reads, writes, inc)

    def act(self, out, in_, func, reads, writes, scale=None, bias=None):
        kw = {}
        if scale is not None:
            kw["scale"] = scale
        if bias is not None:
            kw["bias"] = bias
        return self.op("act", lambda E: E.activation(out=out, in_=in_, func=func, **kw), reads, writes)

    def tt(self, eng, out, in0, in1, op, reads, writes):
        return self.op(eng, lambda E: E.tensor_tensor(out=out, in0=in0, in1=in1, op=op), reads, writes)

    def ts(self, eng, out, in0, s1, s2, op0, op1, reads, writes):
        if op1 is None:
            return self.op(eng, lambda E: E.tensor_scalar(out=out, in0=in0, scalar1=s1, scalar2=None, op0=op0),
                           reads, writes)
        return self.op(eng, lambda E: E.tensor_scalar(out=out, in0=in0, scalar1=s1, scalar2=s2, op0=op0, op1=op1),
                       reads, writes)

    def stt(self, out, in0, scalar, in1, op0, op1, reads, writes):
        return self.op("dve", lambda E: E.scalar_tensor_tensor(out=out, in0=in0, scalar=scalar, in1=in1,
                                                                 op0=op0, op1=op1), reads, writes)

    def copy(self, eng, out, in_, reads, writes):
        return self.op(eng, lambda E: E.tensor_copy(out=out, in_=in_), reads, writes)

    def memset(self, eng, ap, val, writes):
        return self.op(eng, lambda E: E.memset(ap, val), (), writes)


def make_plan(fused):
    if fused:
        s0 = dict(layer=0, tiles=[(-9, "kv3"), (-8, "kv3"), (-7, "kv3"), (-6, "kvall")] +
                  [(i, "full") for i in range(-5, 8)], src="xin", dst="scr", final=False)
        s1 = dict(layer=1, tiles=[(-5, "kv3"), (-4, "kv3"), (-3, "kv3"), (-2, "kvall")] +
                  [(i, "full") for i in range(-1, 8)], src="scr", dst="out", final=True)
        return [s0, s1], -9
    return None, None


def build_program(stages, tmin, n_in_tiles, n_layers_w=2, n_out_tiles=8, dbg=None):
    nc = bass.Bass("TRN2", target_bir_lowering=False)
    P = PB(nc)
    P.limit = dbg
    NTIN = n_in_tiles * T
    LW = n_layers_w

    def dram_in(name, shape):
        return nc.dram_tensor(name, list(shape), F32, kind="ExternalInput").ap()

    xin = dram_in("xin", [D, NTIN])
    w_in = dram_in("w_in", [LW, D, INW])
    w_conf_out = dram_in("w_conf_out", [LW, 768, D])
    w_sc_out = dram_in("w_sc_out", [LW, 768, D])
    w_attn_out = dram_in("w_attn_out", [LW, 256, D])
    w_o = dram_in("w_o", [LW, D, D])
    w_up = dram_in("w_up", [LW, D, 2 * DFF])
    w_down = dram_in("w_down", [LW, DFF, D])
    pp_d = dram_in("pp", [LW, 128, PP_N])
    cst_d = dram_in("cst", [128, C_N])
    hv_d = dram_in("hv", [128, 1])
    out_d = nc.dram_tensor("out", [D, n_out_tiles * T], F32, kind="ExternalOutput").ap()
    need_scr = any(s["dst"] == "scr" for s in stages)
    scr_d = nc.dram_tensor("scr", [D, NTIN], F32, kind="Internal").ap() if need_scr else None

    def fm(ap):
        return ap.rearrange("(kc p) t -> p kc t", p=128)

    xs = [P.sb("x0", [128, 8, T], F32), P.sb("x1", [128, 8, T], F32)]
    xds = [Dep(), Dep()]
    xi = {"i": 0}
    scr_all = Dep()
    h = P.sb("h", [128, 8, T], BF16)
    hd = Dep()
    qT = P.sb("qT", [128, 6, T], BF16)
    qd = Dep()
    kT1 = P.sb("kT1", [128, 2, 2, T], BF16)
    kT2 = P.sb("kT2", [128, 2, 2, T], BF16)
    kT3 = P.sb("kT3", [128, 2, 4 * T], BF16)
    k3c = P.sb("k3c", [128, 2, T], BF16)
    k1d, k2d, k3d, k3cd = Dep(), Dep(), Dep(), Dep()
    v1 = P.sb("v1", [128, 2, 4, 4, 65], BF16)
    v2 = P.sb("v2", [128, 2, 4, 4, 65], BF16)
    v3 = P.sb("v3", [128, 2, 16, 4, 65], BF16)
    v1d, v2d, v3d = Dep(), Dep(), Dep()
    NPT = 4
    pT = [P.sb(f"pT{i}", [128, T], BF16) for i in range(NPT)]
    pTd = [Dep() for _ in range(NPT)]
    oT = P.sb("oT", [64, 4, T], BF16)
    od = Dep()
    Y = P.sb("Y", [128, 8, T], F32)
    Yd = [Dep() for _ in range(8)]
    R = P.sb("R", [128, 22 * T], BF16)
    ACC = R[:, 0:16 * T].bitcast(F32).rearrange("p (a b) -> p a b", a=8)
    actT = R[:, :].rearrange("p (a b) -> p a b", a=22)
    Rd = Dep()
    accd = [Dep() for _ in range(8)]
    G = P.sb("G", [128, 8, T], BF16)
    Gd = [Dep() for _ in range(8)]
    a_buf = P.sb("a_buf", [128, 6, T + 30], BF16)
    ad = [Dep() for _ in range(6)]
    cx_buf = P.sb("cx_buf", [128, 6, T + 2], BF16)
    cxd = [Dep() for _ in range(6)]
    b_sc = P.sb("b_sc", [128, 6, T], BF16)
    bsd = [Dep() for _ in range(6)]
    NTF = 4
    tf = [P.sb(f"tf{i}", [128, T], F32) for i in range(NTF)]
    tfd = [Dep() for _ in range(NTF)]
    NDG = 16
    dg = [P.sb(f"dg{i}", [128, 128], BF16) for i in range(NDG)]
    dgd = [Dep() for _ in range(NDG)]
    uhist = P.sb("uhist", [128, 22, 2, 2], F32)
    uhd = [Dep() for _ in range(22)]
    ubhd = [Dep(), Dep()]
    cstb = P.sb("cstb", [128, C_N], BF16)
    cstd = Dep()
    onesb = P.sb("onesb", [128, 128], BF16)
    onesf = P.sb("onesf", [128, 128], F32)
    epsf = P.sb("epsf", [128, 1], F32)
    onescol = P.sb("onescol", [128, 64], BF16)
    constd = Dep()
    pp = P.sb("pp_sb", [128, LW, PP_N], F32)
    ppd = Dep()
    hv = P.sb("hv_sb", [128, 1], F32)
    hvd = Dep()
    slabs = [P.sb(f"slab{i}", [128, SLAB], BF16) for i in range(NSLAB)]
    slabd = [Dep() for _ in range(NSLAB)]
    ps = P.es.enter_context(nc.psum_tensor("ps", [128, 8, T], F32))
    psd = [Dep() for _ in range(8)]

    P.dma("pool", cstb[:, :], cst_d[:, :], "cst", writes=[cstd])
    P.dma("sp", pp[:, :, :], pp_d.rearrange("l p n -> p l n"), "pp", writes=[ppd])
    P.dma("sp", hv[:, :], hv_d[:, :], "hv", writes=[hvd])
    P.memset("dve", onesb[:, :], 1.0, [constd])
    P.memset("dve", onesf[:, :], 1.0, [constd])
    P.memset("dve", epsf[:, :], EPS, [constd])
    P.memset("dve", onescol[:, :], 1.0, [constd])
    def zero_state():
        for t_, d_ in ((kT1, k1d), (kT2, k2d), (kT3, k3d), (k3c, k3cd), (v1, v1d), (v2, v2d), (v3, v3d)):
            P.op("pool", lambda E, tt_=t_: E.memset(tt_[tuple(slice(None) for _ in tt_.shape)], 0.0), (), [d_])
        P.op("pool", lambda E: E.memset(a_buf[:, :, :], 0.0), (), ad)
        P.op("pool", lambda E: E.memset(cx_buf[:, :, :], 0.0), (), cxd)
        P.op("pool", lambda E: E.memset(uhist[:, :, :, :], 0.0), (), uhd)

    zero_state()
    for i_ in range(NPT):
        P.op("pool", lambda E, i_=i_: E.memset(pT[i_][:, :], 0.0), (), [pTd[i_]])

    def wsrc_of(kind):
        return {"in": w_in, "ao": w_attn_out, "co": w_conf_out, "so": w_sc_out, "wo": w_o,
                "up": w_up, "dn": w_down}[kind]

    def slab_parts(desc):
        kind, L = desc[0], desc[1]
        if kind == "in" or kind == "wo":
            c0, ncols = desc[2], desc[3]
            src = wsrc_of(kind)[L].rearrange("(kc p) c -> p kc c", p=128)[:, :, c0:c0 + ncols]
            n = 8 * ncols
            return [(lambda s_, n=n: s_[:, 0:n].rearrange("p (a b) -> p a b", a=8), src)]
        if kind in ("ao", "co", "so"):
            half = desc[2]
            rows, nk = (64, 4) if kind == "ao" else (128, 6)
            src = wsrc_of(kind)[L].rearrange("(kc p) c -> p kc c", p=rows)[:, :, half * 512:(half + 1) * 512]
            n = nk * 512
            return [(lambda s_, n=n, nk=nk, rows=rows: s_[0:rows, 0:n].rearrange("p (a b) -> p a b", a=nk), src)]
        if kind == "up":
            j = desc[2]
            srcu = w_up[L].rearrange("(kc p) c -> p kc c", p=128)
            return [
                (lambda s_: s_[:, 0:2048].rearrange("p (a b) -> p a b", a=8)[:, :, 0:128],
                 srcu[:, :, j * 128:(j + 1) * 128]),
                (lambda s_: s_[:, 0:2048].rearrange("p (a b) -> p a b", a=8)[:, :, 128:256],
                 srcu[:, :, DFF + j * 128:DFF + (j + 1) * 128]),
            ]
        if kind == "dn":
            oc = desc[2]
            srcd = w_down[L].rearrange("(kc p) c -> p kc c", p=128)[:, :, oc * 128:(oc + 1) * 128]
            return [(lambda s_: s_[:, 0:2816].rearrange("p (a b) -> p a b", a=22), srcd)]
        raise ValueError(kind)

    def tile_slabs(L, mode):
        sl = []
        if mode == "kv3":
            return [("in", L, 1280, 256), ("in", L, 2048, 256)]
        if mode == "full":
            sl += [("in", L, 0, 384), ("in", L, 384, 384)]
        sl += [("in", L, 768, 384), ("in", L, 1152, 384)]
        sl += [("in", L, 1536, 256), ("in", L, 1792, 256), ("in", L, 2048, 256)]
        for i in range(2):
            sl += [("in", L, 2304 + i * 384, 384), ("in", L, 3072 + i * 384, 384)]
        for i in range(2):
            if mode == "full":
                sl += [("in", L, 3840 + i * 384, 384)]
            sl += [("in", L, 4608 + i * 384, 384), ("in", L, 5376 + i * 384, 384)]
        if mode != "full":
            return sl
        for br, kind in enumerate(("ao", "co", "so")):
            sl += [("in", L, 6144 + br * 1024 + i * 256, 256) for i in range(4)]
            sl += [(kind, L, 0), (kind, L, 1)]
        sl += [("wo", L, 0, 384), ("wo", L, 384, 384), ("wo", L, 768, 256)]
        sl += [("up", L, j) for j in range(22)]
        sl += [("dn", L, oc) for oc in range(8)]
        return sl

    slab_seq = []
    for st_ in stages:
        for (_, mode_) in st_["tiles"]:
            slab_seq += tile_slabs(st_["layer"], mode_)
    slab_state = {"issued": 0, "consumed": 0}

    def slab_nel(desc):
        kind = desc[0]
        if kind in ("in", "wo"):
            return 8 * desc[3]
        if kind == "ao":
            return 4 * 512
        if kind in ("co", "so"):
            return 6 * 512
        if kind == "up":
            return 2048
        return 2816

    scr_off = {}
    tot = 0
    for d_ in slab_seq:
        if d_ not in scr_off:
            scr_off[d_] = tot
            tot += slab_nel(d_)
    wscr = nc.dram_tensor("wscr", [128, max(tot, 1)], BF16, kind="Internal").ap()
    scr_done = set()
    wst_all = Dep()

    def issue_slab(kk):
        d_ = slab_seq[kk]
        i = kk % NSLAB
        n = slab_nel(d_)
        rows = 64 if d_[0] == "ao" else 128
        off = scr_off[d_]
        if d_ in scr_done:
            P.dma("sp", slabs[i][0:rows, 0:n], wscr[0:rows, off:off + n], f"slabh{i}", reads=[wst_all],
                  writes=[slabd[i]])
        else:
            for (dst_fn, src) in slab_parts(d_):
                P.dma("pool", dst_fn(slabs[i]), src, f"slab{i}", writes=[slabd[i]])
            P.dma("sp", wscr[0:rows, off:off + n], slabs[i][0:rows, 0:n], "wst", reads=[slabd[i]],
                  writes=[wst_all])
            scr_done.add(d_)

    def next_slab(expect, held=0):
        k = slab_state["consumed"]
        assert slab_seq[k] == expect, (k, slab_seq[k], expect)
        lim = min(k - held + NSLAB, len(slab_seq))
        while slab_state["issued"] < lim:
            issue_slab(slab_state["issued"])
            slab_state["issued"] += 1
        assert slab_state["issued"] > k
        slab_state["consumed"] += 1
        i = k % NSLAB
        return slabs[i], slabd[i]

    def win_slab(L, c0, ncols, kind="in", held=0):
        t_, d_ = next_slab((kind, L, c0, ncols), held)
        n = 8 * ncols
        return t_[:, 0:n].rearrange("p (a b) -> p a b", a=8), d_

    bank_rr = {"i": 0}

    def nb(lo=0, hi=8):
        b = lo + bank_rr["i"] % (hi - lo)
        bank_rr["i"] += 1
        return b

    tf_rr = {"i": 0}

    def ntf():
        i = tf_rr["i"] % NTF
        tf_rr["i"] += 1
        return i

    def rmsnorm(L, gcol, out_t, out_deps, lpp, sq=None, sqd=None):
        if sq is None:
            sq, sqd = h, [hd]
        x, xd = xs[xi["i"]], xds[xi["i"]]
        P.act(sq[:, :, :], x[:, :, :], AF.Square, [xd], sqd)
        b = nb()
        for kc in range(8):
            P.mm(ps[:, b, :], onesb[:, :], sq[:, kc, :], kc == 0, kc == 7, list(sqd) + [constd], [psd[b]])
        t0 = ntf()
        P.ts("dve", tf[t0][:, :], ps[:, b, :], 1.0 / D, EPS, ALU.mult, ALU.add, [psd[b]], [tfd[t0]])
        P.act(tf[t0][:, :], tf[t0][:, :], AF.Sqrt, [tfd[t0]], [tfd[t0]])
        P.op("dve", lambda E: E.reciprocal(out=tf[t0][:, :], in_=tf[t0][:, :]), [tfd[t0]], [tfd[t0]])
        for kc in range(8):
            P.stt(out_t[:, kc, :], x[:, kc, :], pp[:, lpp, gcol + kc:gcol + kc + 1], tf[t0][:, :],
                  ALU.mult, ALU.mult, [xd, tfd[t0], ppd], out_deps)

    def proj_fm(wv, wd, c_local, rhs_t, rhs_d, bank, nk=8, rows=128):
        for kc in range(nk):
            P.mm(ps[:, bank, :], wv[0:rows, kc, c_local * 128:(c_local + 1) * 128], rhs_t[0:rows, kc, :],
                 kc == 0, kc == nk - 1, [wd, rhs_d], [psd[bank]])

    def resid(oc, b, halo):
        x, xd = xs[xi["i"]], xds[xi["i"]]
        if halo:
            P.stt(x[:, oc, :], ps[:, b, :], hv[:, 0:1], x[:, oc, :], ALU.mult, ALU.add, [psd[b], hvd, xd], [xd])
        else:
            P.tt("dve", x[:, oc, :], ps[:, b, :], x[:, oc, :], ALU.add, [psd[b], xd], [xd])

    def row_tiles(p0, p1):
        out = []
        p = p0
        while p < p1:
            if p % 128 == 0 and p1 - p >= 128:
                n = 128
            elif p % 64 == 0 and p1 - p >= 64:
                n = 64
            else:
                n = 32
            out.append((p, n))
            p += n
        return out

    def ones_col(vdep, sl_, halo, p0=0):
        npart = sl_.shape[0]
        if halo:
            n = sl_.shape[1] * sl_.shape[2]
            ov = onescol[p0:p0 + npart, 0:n].rearrange("p (a b c) -> p a b c", a=sl_.shape[1], c=1)
            P.ts("dve", sl_, ov, hv[p0:p0 + npart, 0:1], None, ALU.mult, None, [hvd, constd], [vdep])
        else:
            P.op("dve", lambda E: E.memset(sl_, 1.0), (), [vdep])

    def x_load(st, tidx, bi):
        tok0_ = (tidx - tmin) * T
        if st["src"] == "xin":
            P.dma("sp", xs[bi][:, :, :], fm(xin)[:, :, tok0_:tok0_ + T], f"xld{bi}", writes=[xds[bi]])
        else:
            P.dma("sp", xs[bi][:, :, :], fm(scr_d)[:, :, tok0_:tok0_ + T], f"xld{bi}", reads=[scr_all],
                  writes=[xds[bi]])

    def emit_tile(st, L, tidx, mode, gn, nxt, prenormed):
        halo = tidx < 0
        x, xd = xs[xi["i"]], xds[xi["i"]]
        tok0 = (tidx - tmin) * T
        par = gn % 2
        rho = 32 * (gn % 4)
        sc = (gn // 4) % 2
        P.mark("load_norm1")
        if not prenormed:
            rmsnorm(L, PP_N1G, h, [hd], L)
        if mode != "full" and nxt is not None:
            x_load(nxt[0], nxt[1], 1 - xi["i"])
        P.mark("qkv_proj")
        full = mode == "full"
        kvall = mode in ("full", "kvall")

        if full:
            for sl in range(2):
                wv, wd = win_slab(L, sl * 384, 384)
                for cl in range(3):
                    c = sl * 3 + cl
                    b = nb(0, 4)
                    proj_fm(wv, wd, cl, h, hd, b)
                    P.act(qT[:, c, :], ps[:, b, :], AF.Copy, [psd[b]], [qd], scale=0.125)
        if kvall:
            kplan = [(768, 384, [0, 1, 2]), (1152, 384, [3, 4, 5])]
        else:
            kplan = [(1280, 256, [4, 5])]
        for (c0, ncol, chunks) in kplan:
            wv, wd = win_slab(L, c0, ncol)
            for cl, c in enumerate(chunks):
                g, pair = c // 2, c % 2
                b = nb(0, 4)
                proj_fm(wv, wd, cl, h, hd, b)
                if g == 0:
                    P.copy("dve", kT1[:, pair, par, :], ps[:, b, :], [psd[b]], [k1d])
                elif g == 1:
                    P.copy("dve", kT2[:, pair, par, :], ps[:, b, :], [psd[b]], [k2d])
                else:
                    P.copy("dve", k3c[:, pair, :], ps[:, b, :], [psd[b]], [k3cd])
        if kvall:
            wv, wd = win_slab(L, 1536, 256)
            for j0 in (0, 2):
                b = nb(0, 4)
                for jj in range(2):
                    j = j0 + jj
                    for kc in range(8):
                        P.mm(ps[:, b, jj * 256:(jj + 1) * 256], h[:, kc, j * 128:(j + 1) * 128], wv[:, kc, :],
                             kc == 0, kc == 7, [hd, wd], [psd[b]])
                P.copy("dve", v1[:, par, j0:j0 + 2, :, 0:64],
                       ps[:, b, :].rearrange("p (a b c) -> p a b c", a=2, b=4), [psd[b]], [v1d])
            ones_col(v1d, v1[:, par, :, :, 64:65], halo)
            wv, wd = win_slab(L, 1792, 256)
            for r0 in (0, 2):
                b = nb(0, 4)
                for rr in range(2):
                    r = r0 + rr
                    for kc in range(8):
                        P.mm(ps[:, b, rr * 256:(rr + 1) * 256], h[:, kc, r:T:4], wv[:, kc, :],
                             kc == 0, kc == 7, [hd, wd], [psd[b]])
                P.copy("dve", v2[:, par, r0:r0 + 2, :, 0:64],
                       ps[:, b, :].rearrange("p (a b c) -> p a b c", a=2, b=4), [psd[b]], [v2d])
            ones_col(v2d, v2[:, par, :, :, 64:65], halo)
        wv, wd = win_slab(L, 2048, 256)
        for r0 in range(0, 16, 2):
            b = nb(0, 4)
            for rr in range(2):
                r = r0 + rr
                for kc in range(8):
                    P.mm(ps[rho:rho + 32, b, rr * 256:(rr + 1) * 256], h[:, kc, r:T:16], wv[:, kc, :],
                         kc == 0, kc == 7, [hd, wd], [psd[b]], tp=(0, rho))
            P.copy("dve", v3[rho:rho + 32, sc, r0:r0 + 2, :, 0:64],
                   ps[rho:rho + 32, b, :].rearrange("p (a b c) -> p a b c", a=2, b=4), [psd[b]], [v3d])
        ones_col(v3d, v3[rho:rho + 32, sc, :, :, 64:65], halo, rho)

        if kvall:
            conf_front(L, full)
            sc_front(L, full)
        if full:
            sc_conv(L)
            conf_conv_dve(L)
        if not full:
            P.copy("pool", kT3[:, :, (gn % 4) * T:(gn % 4 + 1) * T], k3c[:, :, :], [k3cd], [k3d])
        else:
            attention(L, gn, par, rho, sc)
            conf_conv(L)
        P.op("pool", lambda E: E.memset(v3[rho:rho + 32, 1 - sc, :, :, :], 0.0), (), [v3d])
        if not full:
            return
        branch(L, 0)
        conf_back(L)
        branch(L, 1)
        branch(L, 2)
        P.mark("w_o")
        for (c0, ncol) in [(0, 384), (384, 384), (768, 256)]:
            wv, wd = win_slab(L, c0, ncol, kind="wo")
            for cl in range(ncol // 128):
                oc = c0 // 128 + cl
                b = nb()
                proj_fm(wv, wd, cl, h, hd, b)
                resid(oc, b, halo)
        if nxt is not None:
            x_load(nxt[0], nxt[1], 1 - xi["i"])
        P.mark("norm2")
        rmsnorm(L, PP_N2G, h, [hd], L)
        ffn(L, halo)
        did_pre = False
        if nxt is not None:
            P.mark("prenorm")
            xi["i"] = 1 - xi["i"]
            rmsnorm(nxt[2], PP_N1G, h, [hd], nxt[2])
            xi["i"] = 1 - xi["i"]
            did_pre = True
        ffn_down(L, halo)
        P.mark("out")
        bi = xi["i"]
        if st["dst"] == "scr":
            P.dma("sp", fm(scr_d)[:, :, tok0:tok0 + T], x[:, :, :], f"xst{bi}", reads=[xd], writes=[scr_all])
        elif tidx >= 0:
            if st["final"]:
                rmsnorm(L, PP_FG, Y, Yd, 0, sq=actT[:, 0:8, :], sqd=[Rd] + accd)
                P.dma("sp", fm(out_d)[:, :, tidx * T:(tidx + 1) * T], Y[:, :, :], "yst", reads=Yd)
            else:
                P.dma("sp", fm(out_d)[:, :, tidx * T:(tidx + 1) * T], x[:, :, :], f"xst{bi}", reads=[xd])
        return did_pre

    def attention(L, gn, par, rho, sc):
        P.mark("attention")
        items = [(g, hh, ch) for g in range(3) for hh in range(4) for ch in range(2)]
        nt = {"i": 0}

        def bias_ap(g, hh, ch):
            if g < 2:
                c0 = C_B12 + ((g * 2 + ch) * 4 + hh) * 128
                return cstb[:, c0:c0 + 128].unsqueeze(1).broadcast_to([128, 4, 128])
            if ch == 0:
                c0 = C_B3A + ((gn % 4) * 4 + hh) * 32
                return cstb[:, c0:c0 + 32].unsqueeze(1).broadcast_to([128, 16, 32])
            c0 = C_B3B + hh * 32
            return cstb[rho:rho + 32, c0:c0 + 32].unsqueeze(1).broadcast_to([32, 16, 32])

        def scores(g, hh, ch, b):
            pair, hr = hh // 2, 64 * (hh % 2)
            c = 2 * g + pair
            if g < 2:
                P.mm(ps[:, b, :].rearrange("p (u q) -> p u q", u=4), cstb[:, C_ID:C_ID + 128], bias_ap(g, hh, ch),
                     True, False, [cstd], [psd[b]], inc=False)
                for u in range(4):
                    if g == 0:
                        qv = qT[hr:hr + 64, c, u * 128:(u + 1) * 128]
                        if ch == 1:
                            kv = kT1[hr:hr + 64, pair, par, u * 128:(u + 1) * 128]
                        elif u > 0:
                            kv = kT1[hr:hr + 64, pair, par, (u - 1) * 128:u * 128]
                        else:
                            kv = kT1[hr:hr + 64, pair, 1 - par, 384:512]
                        kd = k1d
                    else:
                        qv = qT[hr:hr + 64, c, u:T:4]
                        kv = kT2[hr:hr + 64, pair, par if ch == 1 else 1 - par, u:T:4]
                        kd = k2d
                    P.mm(ps[:, b, u * 128:(u + 1) * 128], kv, qv, False, u == 3, [kd, qd], [psd[b]])
            else:
                if ch == 0:
                    P.mm(ps[:, b, :].rearrange("p (u q) -> p u q", u=16), cstb[:, C_ID:C_ID + 128],
                         bias_ap(g, hh, ch), True, False, [cstd], [psd[b]], inc=False)
                    for r in range(16):
                        P.mm(ps[:, b, r * 32:(r + 1) * 32], kT3[hr:hr + 64, pair, r:4 * T:16],
                             qT[hr:hr + 64, c, r:T:16], False, r == 15, [k3d, qd], [psd[b]])
                else:
                    P.mm(ps[rho:rho + 32, b, :].rearrange("p (u q) -> p u q", u=16),
                         cstb[rho:rho + 32, C_ID + rho:C_ID + rho + 32], bias_ap(g, hh, ch),
                         True, False, [cstd], [psd[b]], inc=False, tp=(rho, rho))
                    for r in range(16):
                        P.mm(ps[rho:rho + 32, b, r * 32:(r + 1) * 32], k3c[hr:hr + 64, pair, r:T:16],
                             qT[hr:hr + 64, c, r:T:16], False, r == 15, [k3cd, qd], [psd[b]], tp=(hr, rho))

        def expo(g, hh, ch, b, pi, bprev=None):
            if g == 2 and ch == 1:
                if rho > 0:
                    P.act(pT[pi][0:rho, :], ps[0:rho, bprev, :], AF.Exp, [psd[bprev]], [pTd[pi]])
                P.act(pT[pi][rho:rho + 32, :], ps[rho:rho + 32, b, :], AF.Exp, [psd[b]], [pTd[pi]])
            else:
                P.act(pT[pi][:, :], ps[:, b, :], AF.Exp, [psd[b]], [pTd[pi]])

        def pv(g, hh, ch, pi):
            nbk = 4 + hh
            if g == 0:
                for u in range(4):
                    if ch == 1:
                        vv = v1[:, par, u, hh, :]
                    elif u > 0:
                        vv = v1[:, par, u - 1, hh, :]
                    else:
                        vv = v1[:, 1 - par, 3, hh, :]
                    P.mm(ps[0:65, nbk, u * 128:(u + 1) * 128], vv, pT[pi][:, u * 128:(u + 1) * 128],
                         ch == 0 and u == 0, False, [v1d, pTd[pi]], [psd[nbk]], inc=(u == 3), sgc=True)
            elif g == 1:
                for u in range(4):
                    vv = v2[:, par if ch == 1 else 1 - par, u, hh, :]
                    P.mm(ps[0:65, nbk, u:T:4], vv, pT[pi][:, u * 128:(u + 1) * 128],
                         False, False, [v2d, pTd[pi]], [psd[nbk]], inc=(u == 3), sgc=True)
            else:
                slot = (1 - sc) if ch == 0 else sc
                for r in range(16):
                    P.mm(ps[0:65, nbk, r:T:16], v3[:, slot, r, hh, :], pT[pi][:, r * 32:(r + 1) * 32],
                         False, (ch == 1 and r == 15), [v3d, pTd[pi]], [psd[nbk]], inc=(r == 15), sgc=True)

        pend = []
        last_b = None
        for (g, hh, ch) in items:
            P.mark(f"att_g{g}")
            b = nb(0, 4)
            pi = nt["i"] % NPT
            nt["i"] += 1
            scores(g, hh, ch, b)
            expo(g, hh, ch, b, pi, last_b)
            last_b = b
            pend.append((g, hh, ch, pi))
            if len(pend) > 2:
                pv(*pend.pop(0))
        P.copy("pool", kT3[:, :, (gn % 4) * T:(gn % 4 + 1) * T], k3c[:, :, :], [k3cd], [k3d])
        while pend:
            pv(*pend.pop(0))

    def attn_norm():
        P.mark("attn_norm")
        rl = ACC[:, 0:4, :]
        rlb = ACC[:, 4:8, :]
        rld, rlbd = accd[0:4] + [Rd], accd[4:8] + [Rd]
        P.ts("dve", rl[64:65, :, :], ps[64:65, 4:8, :], 1e-30, None, ALU.add, None, psd[4:8], rld)
        P.op("dve", lambda E: E.reciprocal(out=rl[64:65, :, :], in_=rl[64:65, :, :]), rld, rld)
        for hh in range(4):
            P.mm(ps[0:64, hh, :], onesf[64:65, 0:64], rl[64:65, hh, :], True, True, [constd] + rld, [psd[hh]])
        P.act(rlb[0:64, :, :], ps[0:64, 0:4, :], AF.Copy, psd[0:4], rlbd)
        P.tt("dve", oT[:, :, :], ps[0:64, 4:8, :], rlb[0:64, :, :], ALU.mult, psd[4:8] + rlbd, [od])

    def conf_front(L, full):
        P.mark("conf_front")
        for c in range(6):
            P.copy("pool", a_buf[:, c, 0:30], a_buf[:, c, T:T + 30], [ad[c]], [ad[c]])
        for sl in range(2):
            wva, wda = win_slab(L, 2304 + sl * 384, 384)
            wvg, wdg = win_slab(L, 3072 + sl * 384, 384, held=1)
            for cl in range(3):
                c = sl * 3 + cl
                b1, b2 = nb(), nb()
                proj_fm(wvg, wdg, cl, h, hd, b1)
                proj_fm(wva, wda, cl, h, hd, b2)
                t0 = ntf()
                P.act(tf[t0][:, :], ps[:, b1, :], AF.Sigmoid, [psd[b1]], [tfd[t0]])
                P.tt("dve", a_buf[:, c, 30:30 + T], ps[:, b2, :], tf[t0][:, :], ALU.mult, [psd[b2], tfd[t0]], [ad[c]])

    NCV_DVE = 0

    def conf_conv_dve(L):
        P.mark("conf_conv_dve")
        for k in range(31):
            for c in range(NCV_DVE):
                wcol = pp[:, L, PP_CW + c * 31 + k:PP_CW + c * 31 + k + 1]
                if k == 0:
                    P.ts("dve", Y[:, c, :], a_buf[:, c, 0:T], wcol, pp[:, L, PP_CB + c:PP_CB + c + 1],
                         ALU.mult, ALU.add, [ad[c], ppd], [Yd[c]])
                else:
                    P.stt(Y[:, c, :], a_buf[:, c, k:k + T], wcol, Y[:, c, :], ALU.mult, ALU.add,
                          [ad[c], ppd, Yd[c]], [Yd[c]])

    def conf_conv(L):
        P.mark("conf_conv")
        n_ = 0
        for c in range(NCV_DVE, 6):
            b = nb(0, 4)
            for k in range(31):
                i = n_ % NDG
                wcol = pp[:, L, PP_CW + c * 31 + k:PP_CW + c * 31 + k + 1]
                P.ts("dve", dg[i][:, :], cstb[:, C_ID:C_ID + 128], wcol, None, ALU.mult, None, [cstd, ppd], [dgd[i]])
                P.mm(ps[:, b, :], dg[i][:, :], a_buf[:, c, k:k + T], k == 0, k == 30, [dgd[i], ad[c]], [psd[b]],
                     inc=(n_ % 4 == 3 or k == 30))
                n_ += 1
            P.act(Y[:, c, :], ps[:, b, :], AF.Identity, [psd[b], ppd], [Yd[c]], bias=pp[:, L, PP_CB + c:PP_CB + c + 1])

    def conf_back(L):
        P.mark("conf_back")
        b1, b2 = nb(), nb()
        for c in range(6):
            i1, i2 = (2 * c) % NPT, (2 * c + 1) % NPT
            P.act(pT[i1][:, :], Y[:, c, :], AF.Copy, [Yd[c]], [pTd[i1]])
            P.mm(ps[:, b1, :], onesb[:, :], pT[i1][:, :], c == 0, c == 5, [constd, pTd[i1]], [psd[b1]], inc=True)
            P.act(pT[i2][:, :], Y[:, c, :], AF.Square, [Yd[c]], [pTd[i2]])
            P.mm(ps[:, b2, :], onesb[:, :], pT[i2][:, :], c == 0, c == 5, [constd, pTd[i2]], [psd[b2]], inc=True)
        m, rs = Y[:, 6, :], Y[:, 7, :]
        P.ts("dve", m, ps[:, b1, :], 1.0 / 768, None, ALU.mult, None, [psd[b1]], [Yd[6]])
        t0 = ntf()
        P.tt("dve", tf[t0][:, :], m, m, ALU.mult, [Yd[6]], [tfd[t0]])
        P.stt(rs, ps[:, b2, :], 1.0 / 768, tf[t0][:, :], ALU.mult, ALU.subtract, [psd[b2], tfd[t0]], [Yd[7]])
        P.act(rs, rs, AF.Sqrt, [Yd[7], constd], [Yd[7]], bias=epsf[:, 0:1])
        P.op("dve", lambda E: E.reciprocal(out=rs, in_=rs), [Yd[7]], [Yd[7]])
        tz = [ntf(), ntf()]
        for c in range(6):
            t0 = tz[c % 2]
            P.tt("dve", tf[t0][:, :], Y[:, c, :], m, ALU.subtract, [Yd[c], Yd[6]], [tfd[t0]])
            P.tt("dve", tf[t0][:, :], tf[t0][:, :], rs, ALU.mult, [tfd[t0], Yd[7]], [tfd[t0]])
            P.act(qT[:, c, :], tf[t0][:, :], AF.Silu, [tfd[t0], ppd], [qd],
                  scale=pp[:, L, PP_LG + c:PP_LG + c + 1], bias=pp[:, L, PP_LB + c:PP_LB + c + 1])

    def sc_front(L, full):
        P.mark("sc_front")
        for c in range(6):
            P.copy("pool", cx_buf[:, c, 0:2], cx_buf[:, c, T:T + 2], [cxd[c]], [cxd[c]])
        for sl in range(2):
            if full:
                wvb, wdb = win_slab(L, 3840 + sl * 384, 384)
            wvc, wdc = win_slab(L, 4608 + sl * 384, 384, held=1 if full else 0)
            wvx, wdx = win_slab(L, 5376 + sl * 384, 384, held=2 if full else 1)
            for cl in range(3):
                c = sl * 3 + cl
                if full:
                    b0 = nb()
                    proj_fm(wvb, wdb, cl, h, hd, b0)
                    P.act(b_sc[:, c, :], ps[:, b0, :], AF.Copy, [psd[b0]], [bsd[c]])
                b1, b2 = nb(), nb()
                proj_fm(wvc, wdc, cl, h, hd, b1)
                proj_fm(wvx, wdx, cl, h, hd, b2)
                t0 = ntf()
                P.act(tf[t0][:, :], ps[:, b1, :], AF.Copy, [psd[b1]], [tfd[t0]])
                P.tt("dve", cx_buf[:, c, 2:2 + T], ps[:, b2, :], tf[t0][:, :], ALU.mult, [psd[b2], tfd[t0]], [cxd[c]])

    def sc_conv(L):
        P.mark("sc_conv")
        for c0 in (0, 2, 4):
            tz = [ntf(), ntf()]
            for k in range(3):
                for cc in range(2):
                    c = c0 + cc
                    t0 = tz[cc]
                    wcol = pp[:, L, PP_SW + c * 3 + k:PP_SW + c * 3 + k + 1]
                    if k == 0:
                        P.ts("dve", tf[t0][:, :], cx_buf[:, c, 0:T], wcol, None, ALU.mult, None, [cxd[c], ppd], [tfd[t0]])
                    else:
                        P.stt(tf[t0][:, :], cx_buf[:, c, k:k + T], wcol, tf[t0][:, :], ALU.mult, ALU.add,
                              [cxd[c], ppd, tfd[t0]], [tfd[t0]])
            for cc in range(2):
                c = c0 + cc
                P.tt("dve", b_sc[:, c, :], b_sc[:, c, :], tf[tz[cc]][:, :], ALU.mult, [bsd[c], tfd[tz[cc]]], [bsd[c]])

    def branch(L, br):
        P.mark("branch")
        for sl in range(4):
            wv, wd = win_slab(L, 6144 + br * 1024 + sl * 256, 256)
            for cl in range(2):
                oc = sl * 2 + cl
                b = nb(0, 4) if br == 0 else nb()
                proj_fm(wv, wd, cl, h, hd, b)
                P.act(G[:, oc, :], ps[:, b, :], AF.Sigmoid, [psd[b]], [Gd[oc]])
        if br == 0:
            attn_norm()
            P.mark("branch")
            bkind, rows, nk, inp, inpd = "ao", 64, 4, oT, [od]
        elif br == 1:
            bkind, rows, nk, inp, inpd = "co", 128, 6, qT, [qd]
        else:
            bkind, rows, nk, inp, inpd = "so", 128, 6, b_sc, bsd
        for half in range(2):
            n = nk * 512
            t_, wd = next_slab((bkind, L, half))
            wv = t_[0:rows, 0:n].rearrange("p (a b) -> p a b", a=nk)
            for cl in range(4):
                oc = half * 4 + cl
                b = nb()
                for kc in range(nk):
                    P.mm(ps[:, b, :], wv[:, kc, cl * 128:(cl + 1) * 128], inp[0:rows, kc, :],
                         kc == 0, kc == nk - 1, [wd] + list(inpd), [psd[b]])
                if br == 0:
                    P.tt("dve", ACC[:, oc, :], ps[:, b, :], G[:, oc, :], ALU.mult, [psd[b], Gd[oc]], [accd[oc], Rd])
                else:
                    t0 = ntf()
                    P.tt("dve", tf[t0][:, :], ps[:, b, :], G[:, oc, :], ALU.mult, [psd[b], Gd[oc]], [tfd[t0]])
                    if br == 1:
                        P.tt("dve", ACC[:, oc, :], ACC[:, oc, :], tf[t0][:, :], ALU.add, [accd[oc], tfd[t0]],
                             [accd[oc], Rd])
                    else:
                        P.tt("dve", h[:, oc, :], ACC[:, oc, :], tf[t0][:, :], ALU.add, [accd[oc], tfd[t0]], [hd])

    def ffn(L, halo):
        P.mark("ffn")
        ub = [Y[:, 0:4, :], Y[:, 4:8, :]]
        fw = PP_FW
        fpages = [tf[i][:, :] for i in range(4)] + \
                 [G[:, 2 * i:2 * i + 2, :].rearrange("p a b -> p (a b)").bitcast(F32) for i in range(4)]
        fpaged = [[tfd[i]] for i in range(4)] + [[Gd[2 * i], Gd[2 * i + 1]] for i in range(4)]
        for j in range(22):
            s = j % 2
            ubv = ub[s].rearrange("p a b -> p (a b)").rearrange("p (e n) -> p e n", e=2)
            ubm2 = [[Yd[4 * s], Yd[4 * s + 1]], [Yd[4 * s + 2], Yd[4 * s + 3]]]
            ubh = ubhd[s]
            t_, wd = next_slab(("up", L, j))
            wv = t_[:, 0:2048].rearrange("p (a b) -> p a b", a=8)
            bg, bv = nb(), nb()
            proj_fm(wv, wd, 0, h, hd, bg)
            proj_fm(wv, wd, 1, h, hd, bv)
            P.copy("pool", ubv[:, :, 0:2], uhist[:, j, :, :], [uhd[j]],
                   [ubh] + (list(Yd[4 * s:4 * s + 4]) if j < 2 else []))
            P.act(ubv[:, 0, 2:2 + T], ps[:, bg, :], AF.Copy, [psd[bg]], ubm2[0])
            P.act(ubv[:, 1, 2:2 + T], ps[:, bv, :], AF.Copy, [psd[bv]], ubm2[1])
            P.copy("pool", uhist[:, j, :, :], ubv[:, :, T:T + 2], ubm2[0] + ubm2[1], [uhd[j]])
            pg_t, pg_d = fpages[(2 * j) % 8], fpaged[(2 * j) % 8]
            pv_t, pv_d = fpages[(2 * j + 1) % 8], fpaged[(2 * j + 1) % 8]
            P.act(pg_t, ps[:, bg, :], AF.Copy, [psd[bg], ppd], pg_d,
                  scale=pp[:, L, fw + j * 3 + 2:fw + j * 3 + 3])
            P.act(pv_t, ps[:, bv, :], AF.Copy, [psd[bv], ppd], pv_d,
                  scale=pp[:, L, fw + (22 + j) * 3 + 2:fw + (22 + j) * 3 + 3])
            for k in (1, 0):
                for e, tq, tqd, ch in ((0, pg_t, pg_d, j), (1, pv_t, pv_d, 22 + j)):
                    P.stt(tq, ubv[:, e, k:k + T], pp[:, L, fw + ch * 3 + k:fw + ch * 3 + k + 1], tq,
                          ALU.mult, ALU.add, ubm2[e] + [ubh, ppd] + tqd, tqd)
            P.act(pg_t, pg_t, AF.Silu, pg_d, pg_d)
            P.tt("dve", actT[:, j, :], pg_t, pv_t, ALU.mult, pg_d + pv_d, [Rd] + accd)

    def ffn_down(L, halo):
        P.mark("down")
        for oc in range(8):
            t_, wd = next_slab(("dn", L, oc))
            wv = t_[:, 0:2816].rearrange("p (a b) -> p a b", a=22)
            b = nb()
            for j in range(22):
                P.mm(ps[:, b, :], wv[:, j, :], actT[:, j, :], j == 0, j == 21, [wd, Rd], [psd[b]])
            resid(oc, b, halo)

    flat = []
    for si_, st in enumerate(stages):
        for gn, (tidx, mode) in enumerate(st["tiles"]):
            flat.append((si_, st, tidx, mode, gn))
    x_load(flat[0][1], flat[0][2], 0)
    pre = False
    for n_, (si_, st, tidx, mode, gn) in enumerate(flat):
        if gn == 0 and si_ > 0:
            zero_state()
        xi["i"] = n_ % 2
        nxt = (flat[n_ + 1][1], flat[n_ + 1][2], flat[n_ + 1][1]["layer"]) if n_ + 1 < len(flat) else None
        pre = bool(emit_tile(st, st["layer"], tidx, mode, gn, nxt, pre))
    for e_, E_ in P.engs.items():
        for k_, v_ in P.cnt.items():
            if v_ > 0 and k_ != e_:
                E_.wait_ge(P.sem[k_], v_)
    nc.all_engine_barrier()
    nc.clear_and_free_semaphores(list(P.sem.values()))
    nc.all_engine_barrier()
    P.es.close()
    return nc, P


def _pack_pp(inp, L):
    def v6(a):
        return a.reshape(-1, 128).T
    cols = [v6(inp["norm1_g"][L]), v6(inp["norm2_g"][L]),
            inp["conf_dw_w"][L].reshape(31, 6, 128).transpose(2, 1, 0).reshape(128, 186),
            v6(inp["conf_dw_b"][L]), v6(inp["conf_ln_g"][L]), v6(inp["conf_ln_b"][L]),
            inp["sc_dw_w"][L].reshape(3, 6, 128).transpose(2, 1, 0).reshape(128, 18),
            inp["ffn_dw_w"][L].reshape(3, 44, 128).transpose(2, 1, 0).reshape(128, 132),
            v6(inp["final_g"])]
    out = np.concatenate(cols, axis=1).astype(np.float32)
    assert out.shape == (128, PP_N)
    return out


def _make_cst():
    c = np.zeros((128, C_N), np.float32)
    c[:, C_ID:C_ID + 128] = np.eye(128, dtype=np.float32)
    slopes = 2.0 ** (-8.0 * np.arange(1, 13, dtype=np.float64) / 12.0)
    ki = np.arange(128)[:, None]
    qi = np.arange(128)[None, :]
    stepsA = qi + 128 - ki
    stepsB = qi - ki
    for g in range(3):
        d = DIL[g]
        for hh in range(4):
            sl = slopes[g * 4 + hh]
            bA = np.where(stepsA <= 128, -sl * d * stepsA, NEG)
            bB = np.where(stepsB >= 0, -sl * d * stepsB, NEG)
            if g < 2:
                c0 = C_B12 + ((g * 2 + 0) * 4 + hh) * 128
                c[:, c0:c0 + 128] = bA
                c0 = C_B12 + ((g * 2 + 1) * 4 + hh) * 128
                c[:, c0:c0 + 128] = bB
            else:
                for ri in range(4):
                    rows = (np.arange(128) - 32 * ri) % 128
                    c0 = C_B3A + (ri * 4 + hh) * 32
                    c[:, c0:c0 + 32] = bA[rows, 0:32]
                c0 = C_B3B + hh * 32
                c[:, c0:c0 + 32] = bB[np.arange(128) % 32, 0:32]
    return c


def _weights_map(inp):
    f = lambda a: np.ascontiguousarray(np.asarray(a, dtype=np.float32))
    m = {k: f(inp[k]) for k in ("w_in", "w_conf_out", "w_sc_out", "w_attn_out", "w_o", "w_up", "w_down")}
    m["pp"] = np.stack([_pack_pp(inp, L) for L in range(2)], axis=0)
    m["cst"] = _make_cst()
    return m


def _core_xin(xfull, core, tmin, n_in_tiles):
    b, s = core // 4, (core % 4) * SEG
    lo = s + tmin * T
    hi = lo + n_in_tiles * T
    out = np.zeros((n_in_tiles * T, D), np.float32)
    a = max(lo, 0)
    out[a - lo:hi - lo] = xfull[b, a:hi]
    return np.ascontiguousarray(out.T)


_CACHE = {}


def _run(stages, tmin, n_in_tiles, xfull, wm):
    key = repr((stages, tmin, n_in_tiles))
    if key not in _CACHE:
        _CACHE[key] = build_program(stages, tmin, n_in_tiles)[0]
    nc = _CACHE[key]
    in_maps = []
    for core in range(NCORES):
        m = dict(wm)
        m["xin"] = _core_xin(xfull, core, tmin, n_in_tiles)
        m["hv"] = np.full((128, 1), 0.0 if core % 4 == 0 else 1.0, np.float32)
        in_maps.append(m)
    res = run_bass_kernel_spmd(nc, in_maps, core_ids=list(range(NCORES)))
    out = np.zeros((2, SEQ, D), np.float32)
    for core in range(NCORES):
        b, s = core // 4, (core % 4) * SEG
        out[b, s:s + SEG] = res.results[core]["out"].T
    return out


FUSED = True


def kernel(**inputs):
    inp = {k: np.asarray(v) for k, v in inputs.items()}
    x = np.ascontiguousarray(inp["x"], dtype=np.float32)
    wm = _weights_map(inp)
    if FUSED:
        stages, tmin = make_plan(True)
        return _run(stages, tmin, 17, x, wm)
    tiles = [(-5, "kv3"), (-4, "kv3"), (-3, "kv3"), (-2, "kvall")] + [(i, "full") for i in range(-1, 8)]
    s0 = [dict(layer=0, tiles=tiles, src="xin", dst="out", final=False)]
    s1 = [dict(layer=1, tiles=tiles, src="xin", dst="out", final=True)]
    x1 = _run(s0, -5, 13, x, wm)
    return _run(s1, -5, 13, x1, wm)
```
